# Optimizing a Trainium2 kernel written in Bass

```python
import math
import jax, jax.numpy as jnp
from jax import lax
import numpy as np

D_MODEL = 2048
BATCH = 4
SEQ = 4096
DEPTH = 1

HEAD_DIM = 128
N_ATTN_HEADS = 8
ATTN_WIDTH = N_ATTN_HEADS * HEAD_DIM
MOBA_BLOCK = 256
MOBA_TOPK = 3
Q_CHUNK = 32
ROPE_THETA = 10000.0
SSM_WIDTH = D_MODEL // 2
SSM_GROUP = 16
SSM_GROUPS = SSM_WIDTH // SSM_GROUP
SSM_STATE = 64
FFN_HIDDEN = 5632
CONV_WIDTH = 3
N_BRANCHES = 2
N_MOD = 6
IN_WIDTH = 3 * ATTN_WIDTH + SSM_WIDTH + N_BRANCHES * D_MODEL
EPS = 1e-6
NEG_INF = -1e30

kernel_name = 'moba_s5_gated_hybrid_block'


def rms_norm(x, g):
    xf = x.astype(jnp.float32)
    y = xf * lax.rsqrt(jnp.mean(xf * xf, axis=-1, keepdims=True) + EPS)
    return y.astype(x.dtype) * g


def rotary(x, positions):
    half = HEAD_DIM // 2
    inv_freq = ROPE_THETA ** (-jnp.arange(half, dtype=jnp.float32) / half)
    ang = positions.astype(jnp.float32)[..., None] * inv_freq
    cos = jnp.cos(ang)[:, :, None, :].astype(x.dtype)
    sin = jnp.sin(ang)[:, :, None, :].astype(x.dtype)
    x1, x2 = x[..., :half], x[..., half:]
    return jnp.concatenate([x1 * cos - x2 * sin, x1 * sin + x2 * cos], axis=-1)


def moba_attention(q, k, v):
    b, h, s, dh = q.shape
    n_blocks = -(-s // MOBA_BLOCK)
    s_pad = n_blocks * MOBA_BLOCK
    topk = min(MOBA_TOPK, n_blocks)
    pad = ((0, 0), (0, 0), (0, s_pad - s), (0, 0))
    k_p = jnp.pad(k, pad)
    v_p = jnp.pad(v, pad)
    k_blocks = k_p.reshape(b, h, n_blocks, MOBA_BLOCK, dh)
    v_blocks = v_p.reshape(b, h, n_blocks, MOBA_BLOCK, dh)
    k_mean = jnp.mean(k_blocks.astype(jnp.float32), axis=3).astype(k.dtype)
    scale = HEAD_DIM ** -0.5
    gather = jax.vmap(jax.vmap(lambda blocks, idx: blocks[idx]))

    def chunk(ci):
        start = ci * Q_CHUNK
        q_c = lax.dynamic_slice_in_dim(q, start, Q_CHUNK, axis=2)
        t = start + jnp.arange(Q_CHUNK)
        cur = start // MOBA_BLOCK
        blk_scores = jnp.einsum('bhqd,bhnd->bhqn', q_c, k_mean).astype(jnp.float32)
        past = jnp.arange(n_blocks) < cur
        blk_scores = jnp.where(past, blk_scores, NEG_INF)
        _, idx = lax.top_k(blk_scores, topk)
        slot_ok = jnp.arange(topk) < cur
        k_sel = gather(k_blocks, idx)
        v_sel = gather(v_blocks, idx)
        s_sel = jnp.einsum('bhqd,bhqnjd->bhqnj', q_c, k_sel).astype(jnp.float32) * scale
        s_sel = jnp.where(slot_ok[:, None], s_sel, NEG_INF).reshape(b, h, Q_CHUNK, topk * MOBA_BLOCK)
        k_own = lax.dynamic_slice_in_dim(k_p, cur * MOBA_BLOCK, MOBA_BLOCK, axis=2)
        v_own = lax.dynamic_slice_in_dim(v_p, cur * MOBA_BLOCK, MOBA_BLOCK, axis=2)
        s_own = jnp.einsum('bhqd,bhjd->bhqj', q_c, k_own).astype(jnp.float32) * scale
        key_pos = cur * MOBA_BLOCK + jnp.arange(MOBA_BLOCK)
        s_own = jnp.where(key_pos[None, :] <= t[:, None], s_own, NEG_INF)
        p = jax.nn.softmax(jnp.concatenate([s_sel, s_own], axis=-1), axis=-1).astype(v.dtype)
        p_sel = p[..., :topk * MOBA_BLOCK].reshape(b, h, Q_CHUNK, topk, MOBA_BLOCK)
        p_own = p[..., topk * MOBA_BLOCK:]
        return (jnp.einsum('bhqnj,bhqnjd->bhqd', p_sel, v_sel)
                + jnp.einsum('bhqj,bhjd->bhqd', p_own, v_own))

    out = lax.map(chunk, jnp.arange(s // Q_CHUNK))
    return out.transpose(1, 2, 0, 3, 4).reshape(b, h, s, dh)


def s5_branch(u, lam_re, lam_im, log_dt, b_re, b_im, c_re, c_im, d_skip, w_glu, b_glu):
    bsz, s, _ = u.shape
    f32 = jnp.float32
    ug = u.reshape(bsz, s, SSM_GROUPS, SSM_GROUP).astype(f32)
    dt = jnp.exp(log_dt.astype(f32))[:, None]
    lr = lam_re.astype(f32)
    li = lam_im.astype(f32)
    mag = jnp.exp(lr * dt)
    ab_re = mag * jnp.cos(li * dt)
    ab_im = mag * jnp.sin(li * dt)
    den = lr * lr + li * li
    nr = ab_re - 1.0
    ni = ab_im
    coef_re = (nr * lr + ni * li) / den
    coef_im = (ni * lr - nr * li) / den
    bu_re = jnp.einsum('bsgh,gph->bsgp', ug, b_re.astype(f32))
    bu_im = jnp.einsum('bsgh,gph->bsgp', ug, b_im.astype(f32))
    x_re0 = coef_re * bu_re - coef_im * bu_im
    x_im0 = coef_re * bu_im + coef_im * bu_re
    a_re = jnp.broadcast_to(ab_re, x_re0.shape)
    a_im = jnp.broadcast_to(ab_im, x_re0.shape)

    def combine(e1, e2):
        a1r, a1i, b1r, b1i = e1
        a2r, a2i, b2r, b2i = e2
        return (a2r * a1r - a2i * a1i, a2r * a1i + a2i * a1r,
                a2r * b1r - a2i * b1i + b2r, a2r * b1i + a2i * b1r + b2i)

    _, _, xr, xi = lax.associative_scan(combine, (a_re, a_im, x_re0, x_im0), axis=1)
    y = (jnp.einsum('bsgp,ghp->bsgh', xr, c_re.astype(f32))
         - jnp.einsum('bsgp,ghp->bsgh', xi, c_im.astype(f32))
         + d_skip.astype(f32).reshape(SSM_GROUPS, SSM_GROUP) * ug)
    y = jax.nn.gelu(y.reshape(bsz, s, SSM_WIDTH)).astype(u.dtype)
    return y * jax.nn.sigmoid(y @ w_glu + b_glu)


def causal_dwconv(u, w, bias):
    ch = u.shape[-1]
    y = lax.conv_general_dilated(u, w[:, None, :].astype(u.dtype), window_strides=(1,),
                                 padding=[(CONV_WIDTH - 1, 0)],
                                 dimension_numbers=('NWC', 'WIO', 'NWC'),
                                 feature_group_count=ch)
    return y + bias


def setup_inputs(seed: int = 0) -> dict:
    key = jax.random.key(seed)
    ks = jax.random.split(key, 32)
    nrm = jax.random.normal
    L = DEPTH
    x = nrm(ks[0], (BATCH, SEQ, D_MODEL), jnp.float32)
    c = nrm(ks[1], (BATCH, D_MODEL), jnp.float32)
    offsets = jax.random.randint(ks[2], (BATCH, 1), 0, 1024, dtype=jnp.int32)
    positions = offsets + jnp.arange(SEQ, dtype=jnp.int32)[None, :]
    w_mod = nrm(ks[3], (L, D_MODEL, N_MOD * D_MODEL), jnp.float32) * (0.5 * D_MODEL ** -0.5)
    b_mod = nrm(ks[4], (L, N_MOD * D_MODEL), jnp.float32) * 0.01
    norm1_g = 1.0 + 0.02 * nrm(ks[5], (L, D_MODEL), jnp.float32)
    w_in = nrm(ks[6], (L, D_MODEL, IN_WIDTH), jnp.float32) * D_MODEL ** -0.5
    q_norm_g = 1.0 + 0.02 * nrm(ks[7], (L, HEAD_DIM), jnp.float32)
    k_norm_g = 1.0 + 0.02 * nrm(ks[8], (L, HEAD_DIM), jnp.float32)
    n_idx = jnp.arange(SSM_STATE, dtype=jnp.float32)
    ssm_lambda_re = -0.5 + 0.01 * nrm(ks[9], (L, SSM_GROUPS, SSM_STATE), jnp.float32)
    ssm_lambda_im = math.pi * n_idx + 0.01 * nrm(ks[10], (L, SSM_GROUPS, SSM_STATE), jnp.float32)
    ssm_log_dt = jax.random.uniform(ks[11], (L, SSM_GROUPS), jnp.float32, math.log(1e-3), math.log(1e-1))
    b_scale = (2.0 * SSM_GROUP) ** -0.5
    ssm_b_re = nrm(ks[12], (L, SSM_GROUPS, SSM_STATE, SSM_GROUP), jnp.float32) * b_scale
    ssm_b_im = nrm(ks[13], (L, SSM_GROUPS, SSM_STATE, SSM_GROUP), jnp.float32) * b_scale
    c_scale = (2.0 * SSM_STATE) ** -0.5
    ssm_c_re = nrm(ks[14], (L, SSM_GROUPS, SSM_GROUP, SSM_STATE), jnp.float32) * c_scale
    ssm_c_im = nrm(ks[15], (L, SSM_GROUPS, SSM_GROUP, SSM_STATE), jnp.float32) * c_scale
    ssm_d = nrm(ks[16], (L, SSM_WIDTH), jnp.float32)
    w_glu = nrm(ks[17], (L, SSM_WIDTH, SSM_WIDTH), jnp.float32) * SSM_WIDTH ** -0.5
    b_glu = nrm(ks[18], (L, SSM_WIDTH), jnp.float32) * 0.01
    w_attn_br = nrm(ks[19], (L, ATTN_WIDTH, D_MODEL), jnp.float32) * ATTN_WIDTH ** -0.5
    w_ssm_br = nrm(ks[20], (L, SSM_WIDTH, D_MODEL), jnp.float32) * SSM_WIDTH ** -0.5
    w_out = nrm(ks[21], (L, D_MODEL, D_MODEL), jnp.float32) * D_MODEL ** -0.5
    norm2_g = 1.0 + 0.02 * nrm(ks[22], (L, D_MODEL), jnp.float32)
    w_up = nrm(ks[23], (L, D_MODEL, 2 * FFN_HIDDEN), jnp.float32) * D_MODEL ** -0.5
    conv_w = nrm(ks[24], (L, CONV_WIDTH, 2 * FFN_HIDDEN), jnp.float32) * CONV_WIDTH ** -0.5
    conv_b = nrm(ks[25], (L, 2 * FFN_HIDDEN), jnp.float32) * 0.01
    w_down = nrm(ks[26], (L, FFN_HIDDEN, D_MODEL), jnp.float32) * FFN_HIDDEN ** -0.5
    return {'x': x, 'c': c, 'positions': positions, 'w_mod': w_mod, 'b_mod': b_mod,
            'norm1_g': norm1_g, 'w_in': w_in, 'q_norm_g': q_norm_g, 'k_norm_g': k_norm_g,
            'ssm_lambda_re': ssm_lambda_re, 'ssm_lambda_im': ssm_lambda_im, 'ssm_log_dt': ssm_log_dt,
            'ssm_b_re': ssm_b_re, 'ssm_b_im': ssm_b_im, 'ssm_c_re': ssm_c_re, 'ssm_c_im': ssm_c_im,
            'ssm_d': ssm_d, 'w_glu': w_glu, 'b_glu': b_glu, 'w_attn_br': w_attn_br,
            'w_ssm_br': w_ssm_br, 'w_out': w_out, 'norm2_g': norm2_g, 'w_up': w_up,
            'conv_w': conv_w, 'conv_b': conv_b, 'w_down': w_down}


def reference(x, c, positions, w_mod, b_mod, norm1_g, w_in, q_norm_g, k_norm_g,
              ssm_lambda_re, ssm_lambda_im, ssm_log_dt, ssm_b_re, ssm_b_im, ssm_c_re, ssm_c_im,
              ssm_d, w_glu, b_glu, w_attn_br, w_ssm_br, w_out, norm2_g, w_up, conv_w, conv_b,
              w_down):
    bsz, s, _ = x.shape
    split_pts = [ATTN_WIDTH, 2 * ATTN_WIDTH, 3 * ATTN_WIDTH, 3 * ATTN_WIDTH + SSM_WIDTH]
    for l in range(DEPTH):
        mod = jax.nn.silu(c) @ w_mod[l] + b_mod[l]
        sh1, sc1, g1, sh2, sc2, g2 = jnp.split(mod, N_MOD, axis=-1)

        h = rms_norm(x, norm1_g[l]) * (1.0 + sc1[:, None]) + sh1[:, None]
        proj = h @ w_in[l]
        q, k, v, u, gates = jnp.split(proj, split_pts, axis=-1)
        q = rotary(rms_norm(q.reshape(bsz, s, N_ATTN_HEADS, HEAD_DIM), q_norm_g[l]), positions)
        k = rotary(rms_norm(k.reshape(bsz, s, N_ATTN_HEADS, HEAD_DIM), k_norm_g[l]), positions)
        v = v.reshape(bsz, s, N_ATTN_HEADS, HEAD_DIM)
        attn = moba_attention(q.transpose(0, 2, 1, 3), k.transpose(0, 2, 1, 3), v.transpose(0, 2, 1, 3))
        y_attn = attn.transpose(0, 2, 1, 3).reshape(bsz, s, ATTN_WIDTH)
        y_ssm = s5_branch(u, ssm_lambda_re[l], ssm_lambda_im[l], ssm_log_dt[l], ssm_b_re[l], ssm_b_im[l],
                          ssm_c_re[l], ssm_c_im[l], ssm_d[l], w_glu[l], b_glu[l])
        gate_a, gate_s = jnp.split(gates, N_BRANCHES, axis=-1)
        merged = (jax.nn.sigmoid(gate_a) * (y_attn @ w_attn_br[l])
                  + jax.nn.sigmoid(gate_s) * (y_ssm @ w_ssm_br[l]))
        x = x + g1[:, None] * (merged @ w_out[l])

        h2 = rms_norm(x, norm2_g[l]) * (1.0 + sc2[:, None]) + sh2[:, None]
        up = causal_dwconv(h2 @ w_up[l], conv_w[l], conv_b[l])
        val, gt = jnp.split(up, 2, axis=-1)
        x = x + g2[:, None] * ((jax.nn.silu(gt) * val) @ w_down[l])
    return x
```

```python
from contextlib import ExitStack
import math
import os
import numpy as np
import concourse.bass as bass
import concourse.mybir as mybir
from concourse.bass_utils import run_bass_kernel_spmd

F32 = mybir.dt.float32
BF16 = mybir.dt.bfloat16
I32 = mybir.dt.int32
ALU = mybir.AluOpType
AF = mybir.ActivationFunctionType
AX = mybir.AxisListType

D = 2048
S = 4096
NT = 32
W0 = 15
NW = 17
TW0 = W0 * 128
NWT = NW * 128
H = 8
DH = 128
FH = 5632
NF = FH // 128
EPS = 1e-6
BIG = 30000.0
TWO_PI = 2.0 * math.pi
CW1 = 6.28125
CW2 = TWO_PI - CW1
CH = 256
NCHK = S // CH

ENGS = ("pe", "act", "dve", "pool", "sp")
DEBUG = os.environ.get("MK_DEBUG", "")
STOP = int(os.environ.get("MK_STOP", "99"))


class Res:
    __slots__ = ("name", "w", "r", "accum", "sem", "dcnt", "excl")

    def __init__(self, name, accum=False):
        self.excl = False
        self.name = name
        self.w = {}
        self.r = {}
        self.accum = accum
        self.sem = None
        self.dcnt = 0


class _Rec:
    def __init__(self):
        self.call = None

    def __getattr__(self, name):
        def f(*a, **k):
            self.call = (name, a, k)
            return self
        return f


class Prog:
    def __init__(self, nc):
        self.nc = nc
        self.stack = ExitStack()
        self.ops = {e: [] for e in ENGS}
        self.cnt = {e: 0 for e in ENGS}
        self.known = {e: {} for e in ENGS}
        self.sems = {}
        self.dres = []
        self.nres = 0
        for e in ENGS:
            self.sems["c_" + e] = self.stack.enter_context(nc.semaphore("c_" + e))

    def sb(self, name, shape, dt):
        return self.stack.enter_context(self.nc.sbuf_tensor("sb_" + name, list(shape), dt))

    def ps(self, name, shape, dt):
        return self.stack.enter_context(self.nc.psum_tensor("ps_" + name, list(shape), dt))

    def res(self, name, accum=False):
        self.nres += 1
        return Res(f"{name}{self.nres}", accum)

    def _dsem(self, r):
        if r.sem is None:
            r.sem = "d_" + r.name
            self.sems[r.sem] = self.stack.enter_context(self.nc.semaphore(r.sem))
            self.dres.append(r)
        return r.sem

    def op(self, eng, fn, reads=(), writes=(), dma=None):
        xs = [r for r in reads if r.excl]
        if xs:
            writes = list(writes) + [r for r in xs if r not in writes]
            reads = [r for r in reads if not r.excl]
        waits = {}

        def merge(d):
            for k, v in d.items():
                if v > waits.get(k, 0):
                    waits[k] = v

        for r in reads:
            merge(r.w)
        for r in writes:
            if not r.accum:
                merge(r.w)
                merge(r.r)
        own = "c_" + eng
        if eng == "pe":
            waits.pop(own, None)
        kn = self.known[eng]
        wl = []
        for k, v in waits.items():
            if v > kn.get(k, 0):
                kn[k] = v
                wl.append((k, v))
        if dma is None:
            self.cnt[eng] += 1
            tok = (own, self.cnt[eng])
            inc = 1
        else:
            s = self._dsem(dma)
            dma.dcnt += 16
            tok = (s, dma.dcnt)
            inc = 16
        for r in reads:
            if tok[1] > r.r.get(tok[0], 0):
                r.r[tok[0]] = tok[1]
        for r in writes:
            if r.accum:
                if tok[1] > r.w.get(tok[0], 0):
                    r.w[tok[0]] = tok[1]
            else:
                r.w = {tok[0]: tok[1]}
                r.r = {}
        rec = _Rec()
        fn(rec)
        assert rec.call is not None
        self.ops[eng].append((wl, rec.call, tok[0], inc))

    def dma(self, eng, out, in_, sbres, reads=(), writes=(), **kw):
        self.op(eng, lambda e: e.dma_start(out=out, in_=in_, **kw), reads, writes, dma=sbres)

    def barrier(self):
        toks = {"c_" + e: self.cnt[e] for e in ENGS if self.cnt[e] > 0}
        for r in self.dres:
            toks[r.sem] = r.dcnt
        for e in ENGS:
            kn = self.known[e]
            wl = []
            for k, v in toks.items():
                if k == "c_" + e:
                    continue
                if v > kn.get(k, 0):
                    kn[k] = v
                    wl.append((k, v))
            if wl:
                self.ops[e].append((wl, None, None, 0))

    def emit(self):
        nc = self.nc
        ops = self.ops
        sems = self.sems

        def run(ename, e):
            for wl, fn, sname, inc in ops[ename]:
                for k, v in wl:
                    e.wait_ge(sems[k], v)
                if fn is not None:
                    name, a, k = fn
                    ins = getattr(e, name)(*a, **k)
                    ins.then_inc(sems[sname], inc)

        with nc.Block() as block:
            @block.tensor
            def _(e):
                run("pe", e)

            @block.scalar
            def _(e):
                run("act", e)

            @block.vector
            def _(e):
                run("dve", e)

            @block.gpsimd
            def _(e):
                run("pool", e)

            @block.sync
            def _(e):
                run("sp", e)


class Arena:
    def __init__(self, P, nbytes):
        self.t = P.sb("arena", [128, nbytes // 4], F32)
        self.cap = nbytes
        self.off = 0

    def reset(self):
        self.off = 0

    def alloc(self, shape, dt):
        n = 1
        for s_ in shape:
            n *= s_
        esz = 2 if dt == BF16 else 4
        nb = (n * esz + 63) // 64 * 64
        o = self.off
        self.off += nb
        assert self.off <= self.cap, (self.off, self.cap)
        v = self.t[:, o // 4:(o + nb) // 4]
        if dt != F32:
            v = v.bitcast(dt)
        v = v[:, 0:n]
        if len(shape) == 2:
            v = v.rearrange("p (a b) -> p a b", a=shape[0], b=shape[1])
        elif len(shape) == 3:
            v = v.rearrange("p (a b c) -> p a b c", a=shape[0], b=shape[1], c=shape[2])
        return v


def build_program():
    nc = bass.Bass("TRN2", target_bir_lowering=False)
    P = Prog(nc)

    def din(name, shape, dt=F32):
        return nc.dram_tensor(name, list(shape), dt, kind="ExternalInput").ap()

    skind = "ExternalOutput" if DEBUG else "Internal"

    def dscr(name, shape, dt):
        return nc.dram_tensor(name, list(shape), dt, kind=skind).ap()

    xin = din("xin", [S, D])
    posb = din("posb", [128, S], I32)
    cT = din("cT", [128, 16])
    w_mod = din("w_mod", [D, 6 * D])
    bmodT = din("bmodT", [128, 96])
    n1gT = din("n1gT", [128, 16])
    n2gT = din("n2gT", [128, 16])
    w_in = din("w_in", [D, 8192])
    qkg = din("qkg", [128, 2])
    cst = din("cst", [128, 4])
    ident_d = din("ident", [128, 128])
    caus_d = din("caus", [128, 128])
    eoh_d = din("eoh", [16, 16 * 128])
    pmask_d = din("pmask", [128, NW * 16])
    pbias_d = din("pbias", [128, NW * 16])
    lre_b = din("lre_b", [128, 4096])
    lim_b = din("lim_b", [128, 4096])
    ldt_b = din("ldt_b", [128, 4096])
    lre_p = din("lre_p", [128, 32])
    lim_p = din("lim_p", [128, 32])
    ldt_p = din("ldt_p", [128, 32])
    bre_l = din("bre_l", [128, 4096])
    bim_l = din("bim_l", [128, 4096])
    cre_bd = din("cre_bd", [128, 32 * 32])
    cim_bd = din("cim_bd", [128, 32 * 32])
    dT = din("dT", [128, 8])
    w_glu = din("w_glu", [1024, 1024])
    bgluT = din("bgluT", [128, 8])
    w_abr = din("w_abr", [1024, D])
    w_sbr = din("w_sbr", [1024, D])
    w_out = din("w_out", [D, D])
    w_up = din("w_up", [D, 2 * FH])
    cwT = din("cwT", [128, 3 * 88])
    cbT = din("cbT", [128, 88])
    w_down = din("w_down", [FH, D])
    out = nc.dram_tensor("out", [2048, D], F32, kind="ExternalOutput").ap()

    cos_s = dscr("cos_s", [128, S], F32)
    sin_s = dscr("sin_s", [128, S], F32)
    kT_s = dscr("kT_s", [H, 128, S], BF16)
    vT_s = dscr("vT_s", [H, 128, S], BF16)
    uT_s = dscr("uT_s", [8, 128, S], BF16)
    qT_s = dscr("qT_s", [H, 128, NWT], BF16)
    gT_s = dscr("gT_s", [32, 128, NWT], F32)
    yaT_s = dscr("yaT_s", [H, 128, NWT], BF16)
    ysT_s = dscr("ysT_s", [8, 128, NWT], BF16)
    xm_s = dscr("xm_s", [NWT, D], F32)
    h2T_s = dscr("h2T_s", [16, 128, NWT], BF16)
    wus = nc.dram_tensor("wus", [2, 22, 128, 16 * 256], BF16, kind="Internal").ap()
    wds = nc.dram_tensor("wds", [16, 128, NF * 128], BF16, kind="Internal").ap()
    R_cos, R_sin, R_kT, R_vT, R_uT, R_qT, R_gT, R_ya, R_ys, R_xm, R_out = [
        P.res(n, accum=True) for n in ("cos", "sin", "kT", "vT", "uT", "qT", "gT", "ya", "ys", "xm", "out")]

    ident_f = P.sb("ident_f", [128, 128], F32)
    ident_b = P.sb("ident_b", [128, 128], BF16)
    nident_f = P.sb("nident_f", [128, 128], F32)
    ones_b = P.sb("ones_b", [128, 128], BF16)
    caus_b = P.sb("caus_b", [128, 128], BF16)
    eoh_b = P.sb("eoh_b", [16, 16 * 128], BF16)
    cst_sb = P.sb("cst_sb", [128, 4], F32)
    qkg_sb = P.sb("qkg_sb", [128, 2], F32)
    modv = P.sb("modv", [128, 96], F32)
    gs1 = P.sb("gs1", [128, 16], F32)
    gs2 = P.sb("gs2", [128, 16], F32)
    kmean = P.sb("kmean", [128, H * 16], F32)
    kmean_b = P.sb("kmean_b", [128, H * 16], BF16)
    pmask = P.sb("pmask", [128, NW * 16], F32)
    pbias = P.sb("pbias", [128, NW * 16], F32)
    R_const = P.res("const")
    R_const2 = P.res("constb")
    R_mod = P.res("mod")
    R_kmean = P.res("kmean")
    R_kmb = P.res("kmb")

    banks = [P.ps(f"bank{i}", [128, 512], F32) for i in range(8)]
    R_bank = [P.res(f"bank{i}_") for i in range(8)]
    for r_ in R_bank:
        r_.excl = True
    A = Arena(P, 198 * 1024)

    cc_sb = P.sb("cc_sb", [128, 2], F32)
    eps_c = cc_sb[:, 0:1]
    hpi_c = cc_sb[:, 1:2]
    flag = cst_sb[:, 0:1]
    invf2 = cst_sb[:, 1:2]
    sgn = cst_sb[:, 2:3]

    def sh1(dt_):
        return modv[:, 0 * 16 + dt_: 0 * 16 + dt_ + 1]

    def g1(dt_):
        return modv[:, 2 * 16 + dt_: 2 * 16 + dt_ + 1]

    def sh2(dt_):
        return modv[:, 3 * 16 + dt_: 3 * 16 + dt_ + 1]

    def g2(dt_):
        return modv[:, 5 * 16 + dt_: 5 * 16 + dt_ + 1]

    def phase0():
        A.reset()
        P.dma("sp", ident_f[:], ident_d[:, :], R_const, writes=[R_const])
        P.dma("sp", cst_sb[:], cst[:, :], R_const, writes=[R_const])
        P.dma("sp", qkg_sb[:], qkg[:, :], R_const, writes=[R_const])
        P.dma("sp", pmask[:], pmask_d[:, :], R_const, writes=[R_const])
        P.dma("sp", pbias[:], pbias_d[:, :], R_const, writes=[R_const])
        P.dma("pool", caus_b[:], caus_d[:, :], R_const2, writes=[R_const2])
        P.dma("pool", eoh_b[:], eoh_d[:, :], R_const2, writes=[R_const2])
        P.op("dve", lambda e: e.tensor_copy(out=ident_b[:], in_=ident_f[:]), reads=[R_const], writes=[R_const])
        P.op("dve", lambda e: e.tensor_scalar(out=nident_f[:], in0=ident_f[:], scalar1=-1.0, scalar2=None, op0=ALU.mult),
             reads=[R_const], writes=[R_const])
        P.op("dve", lambda e: e.memset(ones_b[:], 1.0), writes=[R_const])
        P.op("dve", lambda e: e.memset(cc_sb[:, 0:1], EPS), writes=[R_const])
        P.op("dve", lambda e: e.memset(cc_sb[:, 1:2], math.pi / 2), writes=[R_const])
        P.op("dve", lambda e: e.memset(kmean[:], 0.0), writes=[R_kmean])

        c_sb = A.alloc([16], F32)
        sc_b = A.alloc([16], BF16)
        bm_sb = A.alloc([96], F32)
        n1g = A.alloc([16], F32)
        n2g = A.alloc([16], F32)
        r_c = P.res("c")
        P.dma("sp", c_sb, cT[:, :], r_c, writes=[r_c])
        P.dma("sp", bm_sb, bmodT[:, :], r_c, writes=[r_c])
        P.dma("sp", n1g, n1gT[:, :], r_c, writes=[r_c])
        P.dma("sp", n2g, n2gT[:, :], r_c, writes=[r_c])
        P.op("act", lambda e: e.activation(out=sc_b, in_=c_sb, func=AF.Silu), reads=[r_c], writes=[r_c])
        wm = [A.alloc([16, 768], BF16) for _ in range(2)]
        r_wm = [P.res("wm") for _ in range(2)]
        psM = banks[0]
        w_mod_v = w_mod.rearrange("(j p) c -> p j c", p=128)
        for ch in range(16):
            b = ch % 2
            P.dma("pool", wm[b], w_mod_v[:, :, ch * 768:(ch + 1) * 768], r_wm[b], writes=[r_wm[b]])
            for ctl in range(6):
                ct = ch * 6 + ctl
                for j in range(16):
                    P.op("pe", lambda e, b=b, ctl=ctl, ct=ct, j=j: e.matmul(
                        psM[:, ct:ct + 1], lhsT=wm[b][:, j, ctl * 128:(ctl + 1) * 128], rhs=sc_b[:, j:j + 1],
                        start=(j == 0), stop=(j == 15)), reads=[r_wm[b], r_c], writes=[R_bank[0]])
        P.op("dve", lambda e: e.tensor_tensor(out=modv[:], in0=psM[:, 0:96], in1=bm_sb, op=ALU.add),
             reads=[R_bank[0], r_c], writes=[R_mod])
        P.op("dve", lambda e: e.scalar_tensor_tensor(out=gs1[:], in0=modv[:, 16:32], scalar=1.0, in1=n1g,
                                                     op0=ALU.add, op1=ALU.mult), reads=[R_mod, r_c], writes=[R_mod])
        P.op("dve", lambda e: e.scalar_tensor_tensor(out=gs2[:], in0=modv[:, 64:80], scalar=1.0, in1=n2g,
                                                     op0=ALU.add, op1=ALU.mult), reads=[R_mod, r_c], writes=[R_mod])

        pos_i = A.alloc([S], I32)
        ang = A.alloc([S], F32)
        ki = A.alloc([S], I32)
        kf = A.alloc([S], F32)
        rr = A.alloc([S], F32)
        tb = A.alloc([S], F32)
        r_t = P.res("ropetmp")
        P.dma("sp", pos_i, posb[:, :], r_t, writes=[r_t])
        P.op("dve", lambda e: e.tensor_copy(out=ang, in_=pos_i), reads=[r_t], writes=[r_t])
        P.op("dve", lambda e: e.tensor_scalar(out=ang, in0=ang, scalar1=invf2, scalar2=None, op0=ALU.mult),
             reads=[r_t, R_const], writes=[r_t])
        P.op("dve", lambda e: e.tensor_scalar(out=kf, in0=ang, scalar1=1.0 / TWO_PI, scalar2=None, op0=ALU.mult),
             reads=[r_t], writes=[r_t])
        P.op("dve", lambda e: e.tensor_copy(out=ki, in_=kf), reads=[r_t], writes=[r_t])
        P.op("dve", lambda e: e.tensor_copy(out=kf, in_=ki), reads=[r_t], writes=[r_t])
        P.op("dve", lambda e: e.scalar_tensor_tensor(out=rr, in0=kf, scalar=-CW1, in1=ang, op0=ALU.mult, op1=ALU.add),
             reads=[r_t], writes=[r_t])
        P.op("dve", lambda e: e.scalar_tensor_tensor(out=rr, in0=kf, scalar=-CW2, in1=rr, op0=ALU.mult, op1=ALU.add),
             reads=[r_t], writes=[r_t])
        P.op("dve", lambda e: e.tensor_scalar(out=rr, in0=rr, scalar1=math.pi, scalar2=-math.pi, op0=ALU.min, op1=ALU.max),
             reads=[r_t], writes=[r_t])
        r_tb = P.res("tb")
        P.op("act", lambda e: e.activation(out=tb, in_=rr, func=AF.Sin), reads=[r_t], writes=[r_tb])
        P.op("dve", lambda e: e.tensor_scalar(out=tb, in0=tb, scalar1=sgn, scalar2=None, op0=ALU.mult),
             reads=[r_tb, R_const], writes=[r_tb])
        P.dma("sp", sin_s[:, :], tb, r_tb, reads=[r_tb], writes=[R_sin])
        P.op("act", lambda e: e.activation(out=kf, in_=rr, func=AF.Abs), reads=[r_t], writes=[r_t])
        r_tc = P.res("tc")
        P.op("act", lambda e: e.activation(out=ang, in_=kf, func=AF.Sin, scale=-1.0, bias=hpi_c), reads=[r_t, R_const],
             writes=[r_tc, r_t])
        P.dma("sp", cos_s[:, :], ang, r_tc, reads=[r_tc], writes=[R_cos])

    def norm_to_hT(src_rows, ntiles, hT, R_hT, gs, shf, tag):
        xt = [A.alloc([D], F32) for _ in range(2)]
        xn = [A.alloc([D], F32) for _ in range(2)]
        junk = A.alloc([D], BF16)
        ss = [A.alloc([1], F32) for _ in range(2)]
        r_xt = [P.res(tag + "xt") for _ in range(2)]
        r_xn = [P.res(tag + "xn") for _ in range(2)]
        r_junk = P.res(tag + "junk")
        r_ss = [P.res(tag + "ss") for _ in range(2)]
        def stage_a(t):
            b = t % 2
            P.dma("sp", xt[b], src_rows(t), r_xt[b], writes=[r_xt[b]])
            P.op("dve", lambda e: e.memset(ss[b], 0.0), writes=[r_ss[b]])
            P.op("act", lambda e: e.activation(out=junk, in_=xt[b], func=AF.Square, accum_out=ss[b]),
                 reads=[r_xt[b]], writes=[r_junk, r_ss[b]])
            P.op("act", lambda e: e.activation(out=ss[b], in_=ss[b], func=AF.Sqrt, scale=1.0 / D, bias=eps_c),
                 reads=[r_ss[b], R_const], writes=[r_ss[b]])
            P.op("dve", lambda e: e.reciprocal(out=ss[b], in_=ss[b]), reads=[r_ss[b]], writes=[r_ss[b]])
            P.op("act", lambda e: e.activation(out=xn[b], in_=xt[b], func=AF.Copy, scale=ss[b]),
                 reads=[r_xt[b], r_ss[b]], writes=[r_xn[b]])
            for g in range(4):
                bk = (t % 2) * 4 + g
                for q in range(4):
                    dt_ = g * 4 + q
                    P.op("pe", lambda e: e.transpose(out=banks[bk][:, q * 128:(q + 1) * 128],
                                                     in_=xn[b][:, dt_ * 128:(dt_ + 1) * 128], identity=ident_f[:]),
                         reads=[r_xn[b], R_const], writes=[R_bank[bk]])

        def stage_b(t):
            for g in range(4):
                bk = (t % 2) * 4 + g
                for q in range(4):
                    dt_ = g * 4 + q
                    o = hT[:, dt_, t * 128:(t + 1) * 128]
                    i_ = banks[bk][:, q * 128:(q + 1) * 128]
                    if g % 2 == 0:
                        P.op("act", lambda e: e.activation(out=o, in_=i_, func=AF.Identity, scale=gs[:, dt_:dt_ + 1],
                                                           bias=shf(dt_)),
                             reads=[R_bank[bk], R_mod], writes=[R_hT[t][0]])
                    else:
                        P.op("dve", lambda e: e.tensor_scalar(out=o, in0=i_, scalar1=gs[:, dt_:dt_ + 1], scalar2=shf(dt_),
                                                              op0=ALU.mult, op1=ALU.add),
                             reads=[R_bank[bk], R_mod], writes=[R_hT[t][1]])

        stage_a(0)
        for t in range(ntiles):
            if t + 1 < ntiles:
                stage_a(t + 1)
            stage_b(t)

    def phase1():
        A.reset()
        hT = A.alloc([16, S], BF16)
        R_hT = [[P.res("hT"), P.res("hT")] for _ in range(NT)]
        mark = A.off
        norm_to_hT(lambda t: xin[t * 128:(t + 1) * 128, :], NT, hT, R_hT, gs1, sh1, "p1")
        P.barrier()
        A.off = mark
        allhT = [r for pr in R_hT for r in pr]
        wt = [A.alloc([16, 256], BF16) for _ in range(2)]
        r_wt = [P.res("wt") for _ in range(2)]
        sqb = [A.alloc([512], BF16) for _ in range(2)]
        rstd = [A.alloc([512], F32) for _ in range(2)]
        qn = [A.alloc([512], F32) for _ in range(2)]
        cs = [A.alloc([512], F32) for _ in range(2)]
        sn = [A.alloc([512], F32) for _ in range(2)]
        ta = [A.alloc([512], F32) for _ in range(2)]
        tbm = [A.alloc([512], F32) for _ in range(2)]
        obq = [A.alloc([512], BF16) for _ in range(2)]
        r_obq = [P.res("obq") for _ in range(2)]
        of = [A.alloc([512], F32) for _ in range(2)]
        ob = [A.alloc([512], BF16) for _ in range(2)]
        og = [A.alloc([512], F32) for _ in range(2)]
        r_sqb = [P.res("sqb") for _ in range(2)]
        r_rstd = [P.res("rstd") for _ in range(2)]
        r_qn = [P.res("qn") for _ in range(2)]
        r_cs = [P.res("cs") for _ in range(2)]
        r_sn = [P.res("sn") for _ in range(2)]
        r_ta = [P.res("ta") for _ in range(2)]
        r_tbm = [P.res("tbm") for _ in range(2)]
        r_of = [P.res("of") for _ in range(2)]
        r_ob = [P.res("ob") for _ in range(2)]
        r_og = [P.res("og") for _ in range(2)]
        w_in_v = w_in.rearrange("(j p) c -> p j c", p=128)
        blks = []
        for c2 in range(32):
            for s_ in range(2):
                ct = c2 * 2 + s_
                if ct < 8:
                    kind, idx = "q", ct
                elif ct < 16:
                    kind, idx = "k", ct - 8
                elif ct < 24:
                    kind, idx = "v", ct - 16
                elif ct < 32:
                    kind, idx = "u", ct - 24
                else:
                    kind, idx = "g", ct - 32
                if kind in ("q", "g"):
                    blocks = [(TW0, 128)] + [(2048 + 512 * i, 512) for i in range(4)]
                else:
                    blocks = [(512 * i, 512) for i in range(8)]
                for bi_, (t0, n) in enumerate(blocks):
                    blks.append((c2, s_, kind, idx, t0, n, s_ == 0 and bi_ == 0))

        def issue_main(bd, it):
            c2, s_, kind, idx, t0, n, first = bd
            wb = c2 % 2
            bk = it % 4
            if first:
                P.dma("pool", wt[wb], w_in_v[:, :, c2 * 256:(c2 + 1) * 256], r_wt[wb], writes=[r_wt[wb]])
            tiles = range(t0 // 128, (t0 + n) // 128)
            rh = [R_hT[t][p_] for t in tiles for p_ in range(2)]
            for j in range(16):
                P.op("pe", lambda e: e.matmul(banks[bk][:, 0:n], lhsT=wt[wb][:, j, s_ * 128:(s_ + 1) * 128],
                                              rhs=hT[:, j, t0:t0 + n], start=(j == 0), stop=(j == 15)),
                     reads=[r_wt[wb]] + rh, writes=[R_bank[bk]])

        def post(bd, it, half_):
            c2, s_, kind, idx, t0, n, first = bd
            b = it % 2
            bk = it % 4
            pso = banks[bk][:, 0:n]
            if kind in ("q", "k") and half_ == "A":
                gcol = qkg_sb[:, 0:1] if kind == "q" else qkg_sb[:, 1:2]
                sbk = 4 + b
                P.op("act", lambda e: e.activation(out=sqb[b][:, 0:n], in_=pso, func=AF.Square),
                     reads=[R_bank[bk]], writes=[r_sqb[b]])
                P.op("pe", lambda e: e.matmul(banks[sbk][:, 0:n], lhsT=ones_b[:], rhs=sqb[b][:, 0:n], start=True, stop=True),
                     reads=[r_sqb[b], R_const], writes=[R_bank[sbk]])
                P.op("act", lambda e: e.activation(out=rstd[b][:, 0:n], in_=banks[sbk][:, 0:n], func=AF.Ln, scale=1.0 / DH,
                                                   bias=eps_c), reads=[R_bank[sbk], R_const], writes=[r_rstd[b]])
                P.op("act", lambda e: e.activation(out=rstd[b][:, 0:n], in_=rstd[b][:, 0:n], func=AF.Exp, scale=-0.5),
                     reads=[r_rstd[b]], writes=[r_rstd[b]])
                P.op("dve", lambda e: e.scalar_tensor_tensor(out=qn[b][:, 0:n], in0=pso, scalar=gcol, in1=rstd[b][:, 0:n],
                                                             op0=ALU.mult, op1=ALU.mult),
                     reads=[R_bank[bk], r_rstd[b], R_const], writes=[r_qn[b]])
                P.dma("sp", cs[b][:, 0:n], cos_s[:, t0:t0 + n], r_cs[b], reads=[R_cos], writes=[r_cs[b]])
                P.dma("sp", sn[b][:, 0:n], sin_s[:, t0:t0 + n], r_sn[b], reads=[R_sin], writes=[r_sn[b]])
                if half_ == "A":
                    return
            if half_ == "A":
                return
            if kind in ("q", "k"):
                P.op("pool", lambda e: e.tensor_tensor(out=ta[b][:, 0:n], in0=qn[b][:, 0:n], in1=cs[b][:, 0:n], op=ALU.mult),
                     reads=[r_qn[b], r_cs[b]], writes=[r_ta[b]])
                P.op("dve", lambda e: e.tensor_tensor(out=tbm[b][0:64, 0:n], in0=qn[b][64:128, 0:n], in1=sn[b][64:128, 0:n],
                                                      op=ALU.mult), reads=[r_qn[b], r_sn[b]], writes=[r_tbm[b]])
                P.op("dve", lambda e: e.tensor_tensor(out=tbm[b][64:128, 0:n], in0=qn[b][0:64, 0:n], in1=sn[b][0:64, 0:n],
                                                      op=ALU.mult), reads=[r_qn[b], r_sn[b]], writes=[r_tbm[b]])
                P.op("pool", lambda e: e.tensor_tensor(out=obq[b][:, 0:n], in0=ta[b][:, 0:n], in1=tbm[b][:, 0:n], op=ALU.add),
                     reads=[r_ta[b], r_tbm[b]], writes=[r_obq[b]])
                if kind == "k":
                    nb0 = t0 // 256
                    P.op("dve", lambda e: e.tensor_reduce(out=kmean[:, idx * 16 + nb0: idx * 16 + nb0 + 2],
                                                          in_=obq[b][:, 0:512].rearrange("p (a t) -> p a t", t=256),
                                                          axis=AX.X, op=ALU.add), reads=[r_obq[b]], writes=[R_kmean])
                    P.dma("sp", kT_s[idx, :, t0:t0 + n], obq[b][:, 0:n], r_obq[b], reads=[r_obq[b]], writes=[R_kT])
                else:
                    P.dma("sp", qT_s[idx, :, t0 - TW0:t0 - TW0 + n], obq[b][:, 0:n], r_obq[b], reads=[r_obq[b]], writes=[R_qT])
            elif kind in ("q", "k"):
                pass
            elif kind in ("v", "u"):
                P.op("act", lambda e: e.copy(out=ob[b][:, 0:n], in_=pso), reads=[R_bank[bk]], writes=[r_ob[b]])
                dst = vT_s if kind == "v" else uT_s
                P.dma("sp", dst[idx, :, t0:t0 + n], ob[b][:, 0:n], r_ob[b], reads=[r_ob[b]],
                      writes=[R_vT if kind == "v" else R_uT])
            else:
                P.op("act", lambda e: e.activation(out=og[b][:, 0:n], in_=pso, func=AF.Sigmoid),
                     reads=[R_bank[bk]], writes=[r_og[b]])
                P.dma("sp", gT_s[idx, :, t0 - TW0:t0 - TW0 + n], og[b][:, 0:n], r_og[b], reads=[r_og[b]], writes=[R_gT])

        nbk = len(blks)
        for it, bd in enumerate(blks):
            issue_main(bd, it)
            if it >= 1:
                post(blks[it - 1], it - 1, "A")
            if it >= 2:
                post(blks[it - 2], it - 2, "B")
        post(blks[nbk - 1], nbk - 1, "A")
        post(blks[nbk - 2], nbk - 2, "B")
        post(blks[nbk - 1], nbk - 1, "B")
        P.op("dve", lambda e: e.tensor_scalar(out=kmean_b[:], in0=kmean[:], scalar1=1.0 / 256, scalar2=None, op0=ALU.mult),
             reads=[R_kmean], writes=[R_kmb])

    def phase2():
        A.reset()
        kT = [A.alloc([S], BF16) for _ in range(2)]
        vT = [A.alloc([S], BF16) for _ in range(2)]
        qT = [A.alloc([NWT], BF16) for _ in range(2)]
        V1 = [A.alloc([NT, 128], BF16) for _ in range(2)]
        r_kT = [P.res("kT") for _ in range(2)]
        r_vT = [P.res("vT") for _ in range(2)]
        r_qT = [P.res("qT") for _ in range(2)]
        r_V1 = [P.res("V1") for _ in range(2)]
        biasq = A.alloc([NW * 16], F32)
        r_biasq = P.res("biasq")
        scm = A.alloc([16], F32)
        top8 = A.alloc([8], F32)
        r_scm = P.res("scm")
        biasT = [A.alloc([NWT], BF16) for _ in range(2)]
        r_biasT = [P.res("biasT") for _ in range(2)]
        pT = [A.alloc([512], BF16) for _ in range(3)]
        r_pT = [P.res("pT") for _ in range(3)]
        rec = [A.alloc([512], F32) for _ in range(2)]
        r_rec = [P.res("rec") for _ in range(2)]
        yaT = [A.alloc([NWT], BF16) for _ in range(2)]
        r_ya = [P.res("yaT") for _ in range(2)]
        tb16 = [banks[6][:, :].bitcast(BF16), banks[7][:, :].bitcast(BF16)]
        scale = DH ** -0.5
        groups = [(0, 1)] + [(1 + 4 * g, 4) for g in range(4)]
        pit = 0
        def pro1(h):
            b = h % 2
            P.dma("sp", kT[b], kT_s[h, :, :], r_kT[b], reads=[R_kT], writes=[r_kT[b]])
            P.dma("sp", vT[b], vT_s[h, :, :], r_vT[b], reads=[R_vT], writes=[r_vT[b]])
            P.dma("sp", qT[b], qT_s[h, :, :], r_qT[b], reads=[R_qT], writes=[r_qT[b]])
            for g in range(4):
                tbk = g % 2
                for q in range(8):
                    t = g * 8 + q
                    P.op("pe", lambda e: e.transpose(out=tb16[tbk][:, q * 128:(q + 1) * 128], in_=vT[b][:, t * 128:(t + 1) * 128],
                                                     identity=ident_b[:]),
                         reads=[r_vT[b], R_const], writes=[R_bank[6 + tbk]])
                src_ = tb16[tbk].rearrange("p (a c) -> p a c", c=128)
                dst_ = V1[b][:, g * 8:(g + 1) * 8, :]
                if g % 2 == 0:
                    P.op("dve", lambda e: e.tensor_copy(out=dst_, in_=src_), reads=[R_bank[6 + tbk]], writes=[r_V1[b]])
                else:
                    P.op("act", lambda e: e.copy(out=dst_, in_=src_), reads=[R_bank[6 + tbk]], writes=[r_V1[b]])

        def pro2(h):
            b = h % 2
            for i in range(NW):
                P.op("pe", lambda e: e.matmul(banks[7][:, 0:16], lhsT=qT[b][:, i * 128:(i + 1) * 128],
                                              rhs=kmean_b[:, h * 16:(h + 1) * 16], start=True, stop=True),
                     reads=[r_qT[b], R_kmb], writes=[R_bank[7]])
                P.op("dve", lambda e: e.tensor_tensor(out=scm, in0=banks[7][:, 0:16], in1=pmask[:, i * 16:(i + 1) * 16],
                                                      op=ALU.add), reads=[R_bank[7], R_const], writes=[r_scm])
                P.op("dve", lambda e: e.max(out=top8, in_=scm), reads=[r_scm], writes=[r_scm])
                P.op("dve", lambda e: e.tensor_scalar(out=scm, in0=scm, scalar1=top8[:, 2:3], scalar2=None, op0=ALU.is_ge),
                     reads=[r_scm], writes=[r_scm])
                P.op("dve", lambda e: e.tensor_scalar(out=scm, in0=scm, scalar1=-1.0, scalar2=BIG, op0=ALU.add, op1=ALU.mult),
                     reads=[r_scm], writes=[r_scm])
                P.op("dve", lambda e: e.tensor_tensor(out=biasq[:, i * 16:(i + 1) * 16], in0=scm,
                                                      in1=pbias[:, i * 16:(i + 1) * 16], op=ALU.add),
                     reads=[r_scm, R_const], writes=[r_biasq])

        def pro3(h):
            b = h % 2
            for g in range(5):
                i0 = g * 4
                ni = min(4, NW - i0)
                for q in range(ni):
                    i = i0 + q
                    P.op("pe", lambda e: e.transpose(out=banks[7][0:16, q * 128:(q + 1) * 128],
                                                     in_=biasq[:, i * 16:(i + 1) * 16], identity=ident_f[:]),
                         reads=[r_biasq, R_const], writes=[R_bank[7]])
                P.op("dve", lambda e: e.tensor_copy(out=biasT[b][0:16, i0 * 128:(i0 + ni) * 128],
                                                    in_=banks[7][0:16, 0:ni * 128]),
                     reads=[R_bank[7]], writes=[r_biasT[b]])

        pro1(0)
        pro2(0)
        pro3(0)
        for h in range(H):
            b = h % 2
            for gi, (i0, ni) in enumerate(groups):
                T0 = W0 + i0
                Tl = T0 + ni - 1
                NQc = ni * 128
                ob_ = gi % 2
                o_ps = banks[2 + ob_]
                s_ps = banks[4 + ob_]
                kts = list(range(Tl + 1))

                def issue_s(kt, slot):
                    c_lo = max(0, kt - T0) * 128
                    nb = kt // 2
                    b_lo = max(0, 2 * nb + 2 - T0)
                    has_bias = b_lo < ni
                    has_caus = T0 <= kt <= Tl
                    P.op("pe", lambda e: e.matmul(banks[slot][:, c_lo:NQc], lhsT=kT[b][:, kt * 128:(kt + 1) * 128],
                                                  rhs=qT[b][:, i0 * 128 + c_lo:i0 * 128 + NQc], start=True,
                                                  stop=not (has_bias or has_caus), skip_group_check=True),
                         reads=[r_kT[b], r_qT[b]], writes=[R_bank[slot]])
                    if has_bias:
                        P.op("pe", lambda e: e.matmul(banks[slot][:, b_lo * 128:NQc], lhsT=eoh_b[0:16, nb * 128:(nb + 1) * 128],
                                                      rhs=biasT[b][0:16, (i0 + b_lo) * 128:(i0 + ni) * 128], start=False,
                                                      stop=not has_caus, skip_group_check=True),
                             reads=[R_const2, r_biasT[b]], writes=[R_bank[slot]])
                    if has_caus:
                        ct = kt - T0
                        P.op("pe", lambda e: e.matmul(banks[slot][:, ct * 128:(ct + 1) * 128], lhsT=ident_b[:], rhs=caus_b[:],
                                                      start=False, stop=True, skip_group_check=True),
                             reads=[R_const, R_const2], writes=[R_bank[slot]])

                issue_s(kts[0], 0)
                for idx_, kt in enumerate(kts):
                    slot = idx_ % 2
                    if idx_ + 1 < len(kts):
                        issue_s(kts[idx_ + 1], (idx_ + 1) % 2)
                    c_lo = max(0, kt - T0) * 128
                    pb = pit % 3
                    pit += 1
                    P.op("act", lambda e: e.activation(out=pT[pb][:, c_lo:NQc], in_=banks[slot][:, c_lo:NQc], func=AF.Exp,
                                                       scale=scale),
                         reads=[R_bank[slot]], writes=[r_pT[pb]])
                    P.op("pe", lambda e: e.matmul(o_ps[:, c_lo:NQc], lhsT=V1[b][:, kt, :], rhs=pT[pb][:, c_lo:NQc],
                                                  start=(kt == 0), stop=(kt == Tl), skip_group_check=True),
                         reads=[r_pT[pb], r_V1[b]], writes=[R_bank[2 + ob_]])
                    P.op("pe", lambda e: e.matmul(s_ps[:, c_lo:NQc], lhsT=ones_b[:], rhs=pT[pb][:, c_lo:NQc],
                                                  start=(kt == 0), stop=(kt == Tl), skip_group_check=True),
                         reads=[r_pT[pb], R_const], writes=[R_bank[4 + ob_]])
                P.op("dve", lambda e: e.reciprocal(out=rec[ob_][:, 0:NQc], in_=s_ps[:, 0:NQc]),
                     reads=[R_bank[4 + ob_]], writes=[r_rec[ob_]])
                P.op("dve", lambda e: e.tensor_tensor(out=yaT[b][:, i0 * 128:(i0 + ni) * 128], in0=o_ps[:, 0:NQc],
                                                      in1=rec[ob_][:, 0:NQc], op=ALU.mult),
                     reads=[R_bank[2 + ob_], r_rec[ob_]], writes=[r_ya[b]])
                if h + 1 < H:
                    if gi == 1:
                        pro1(h + 1)
                    elif gi == 2:
                        pro2(h + 1)
                    elif gi == 3:
                        pro3(h + 1)
            P.dma("sp", yaT_s[h, :, :], yaT[b], r_ya[b], reads=[r_ya[b]], writes=[R_ya])

    def phase3():
        A.reset()
        wBre = A.alloc([32, 128], BF16)
        wBim = A.alloc([32, 128], BF16)
        wCre = A.alloc([32, 32], BF16)
        wCim = A.alloc([32, 32], BF16)
        wD = A.alloc([32, 32], BF16)
        wCreN = A.alloc([32, 32], BF16)
        wCimN = A.alloc([32, 32], BF16)
        d_sb = A.alloc([8], F32)
        p_lr = A.alloc([32], F32)
        p_li = A.alloc([32], F32)
        p_dt = A.alloc([32], F32)
        rho = A.alloc([32], F32)
        th = A.alloc([32], F32)
        p_t = A.alloc([32], F32)
        p_i = A.alloc([32], I32)
        mark3 = A.off
        t_lr = A.alloc([4096], F32)
        t_li = A.alloc([4096], F32)
        t_dt = A.alloc([4096], F32)
        t_a = A.alloc([4096], F32)
        t_b = A.alloc([4096], F32)
        t_c = A.alloc([4096], F32)
        t_d = A.alloc([4096], F32)
        t_e = A.alloc([4096], F32)
        t_i = A.alloc([4096], I32)
        r_s = P.res("s5setup")
        P.dma("sp", t_lr, lre_b[:, :], r_s, writes=[r_s])
        P.dma("sp", t_li, lim_b[:, :], r_s, writes=[r_s])
        P.dma("sp", t_dt, ldt_b[:, :], r_s, writes=[r_s])

        def dv(fn, eng="dve"):
            P.op(eng, fn, reads=[r_s], writes=[r_s])

        def sincos(theta, s_out, c_out, tmp, tmpi, red_out):
            dv(lambda e: e.tensor_scalar(out=tmp, in0=theta, scalar1=1.0 / TWO_PI, scalar2=None, op0=ALU.mult))
            dv(lambda e: e.tensor_copy(out=tmpi, in_=tmp))
            dv(lambda e: e.tensor_copy(out=tmp, in_=tmpi))
            dv(lambda e: e.scalar_tensor_tensor(out=red_out, in0=tmp, scalar=-CW1, in1=theta, op0=ALU.mult, op1=ALU.add))
            dv(lambda e: e.scalar_tensor_tensor(out=red_out, in0=tmp, scalar=-CW2, in1=red_out, op0=ALU.mult, op1=ALU.add))
            dv(lambda e: e.tensor_scalar(out=red_out, in0=red_out, scalar1=math.pi, scalar2=-math.pi, op0=ALU.min, op1=ALU.max))
            dv(lambda e: e.activation(out=s_out, in_=red_out, func=AF.Sin), "act")
            dv(lambda e: e.activation(out=tmp, in_=red_out, func=AF.Abs), "act")
            dv(lambda e: e.activation(out=c_out, in_=tmp, func=AF.Sin, scale=-1.0, bias=hpi_c), "act")

        dv(lambda e: e.activation(out=t_dt, in_=t_dt, func=AF.Exp), "act")
        dv(lambda e: e.tensor_tensor(out=t_a, in0=t_lr, in1=t_dt, op=ALU.mult))
        dv(lambda e: e.activation(out=t_a, in_=t_a, func=AF.Exp), "act")
        dv(lambda e: e.tensor_tensor(out=t_b, in0=t_li, in1=t_dt, op=ALU.mult))
        sincos(t_b, t_c, t_d, t_e, t_i, t_dt)
        dv(lambda e: e.tensor_tensor(out=t_c, in0=t_c, in1=t_a, op=ALU.mult))
        dv(lambda e: e.tensor_tensor(out=t_d, in0=t_d, in1=t_a, op=ALU.mult))
        dv(lambda e: e.tensor_scalar(out=t_d, in0=t_d, scalar1=-1.0, scalar2=None, op0=ALU.add))
        dv(lambda e: e.tensor_tensor(out=t_a, in0=t_lr, in1=t_lr, op=ALU.mult))
        dv(lambda e: e.tensor_tensor(out=t_b, in0=t_li, in1=t_li, op=ALU.mult))
        dv(lambda e: e.tensor_tensor(out=t_a, in0=t_a, in1=t_b, op=ALU.add))
        dv(lambda e: e.reciprocal(out=t_a, in_=t_a))
        dv(lambda e: e.tensor_tensor(out=t_b, in0=t_d, in1=t_lr, op=ALU.mult))
        dv(lambda e: e.tensor_tensor(out=t_e, in0=t_c, in1=t_li, op=ALU.mult))
        dv(lambda e: e.tensor_tensor(out=t_b, in0=t_b, in1=t_e, op=ALU.add))
        dv(lambda e: e.tensor_tensor(out=t_b, in0=t_b, in1=t_a, op=ALU.mult))
        dv(lambda e: e.tensor_tensor(out=t_e, in0=t_c, in1=t_lr, op=ALU.mult))
        dv(lambda e: e.tensor_tensor(out=t_dt, in0=t_d, in1=t_li, op=ALU.mult))
        dv(lambda e: e.tensor_tensor(out=t_e, in0=t_e, in1=t_dt, op=ALU.subtract))
        dv(lambda e: e.tensor_tensor(out=t_e, in0=t_e, in1=t_a, op=ALU.mult))
        P.dma("sp", t_lr, bre_l[:, :], r_s, reads=[r_s], writes=[r_s])
        P.dma("sp", t_li, bim_l[:, :], r_s, reads=[r_s], writes=[r_s])
        r_wB = P.res("wB")
        wBre_f = wBre.rearrange("p a b -> p (a b)")
        wBim_f = wBim.rearrange("p a b -> p (a b)")
        dv(lambda e: e.tensor_tensor(out=t_a, in0=t_lr, in1=t_b, op=ALU.mult))
        dv(lambda e: e.tensor_tensor(out=t_c, in0=t_li, in1=t_e, op=ALU.mult))
        P.op("dve", lambda e: e.tensor_tensor(out=wBre_f, in0=t_a, in1=t_c, op=ALU.subtract), reads=[r_s], writes=[r_wB])
        dv(lambda e: e.tensor_tensor(out=t_a, in0=t_li, in1=t_b, op=ALU.mult))
        dv(lambda e: e.tensor_tensor(out=t_c, in0=t_lr, in1=t_e, op=ALU.mult))
        P.op("dve", lambda e: e.tensor_tensor(out=wBim_f, in0=t_a, in1=t_c, op=ALU.add), reads=[r_s], writes=[r_wB])
        r_wC = P.res("wC")
        P.dma("pool", wCre.rearrange("p a b -> p (a b)"), cre_bd[:, :], r_wC, writes=[r_wC])
        r_wC2 = P.res("wC2")
        P.dma("pool", wCim.rearrange("p a b -> p (a b)"), cim_bd[:, :], r_wC2, writes=[r_wC2])
        r_d = P.res("d")
        P.dma("sp", d_sb, dT[:, :], r_d, writes=[r_d])
        P.op("dve", lambda e: e.tensor_scalar(out=wCreN.rearrange("p a b -> p (a b)"), in0=wCre.rearrange("p a b -> p (a b)"),
                                              scalar1=-1.0, scalar2=None, op0=ALU.mult), reads=[r_wC], writes=[r_wC])
        P.op("dve", lambda e: e.tensor_scalar(out=wCimN.rearrange("p a b -> p (a b)"), in0=wCim.rearrange("p a b -> p (a b)"),
                                              scalar1=-1.0, scalar2=None, op0=ALU.mult), reads=[r_wC2], writes=[r_wC2])
        for pr in range(32):
            j, q = pr // 4, pr % 4
            P.op("dve", lambda e, pr=pr, j=j, q=q: e.tensor_scalar(out=wD[:, pr, :], in0=ident_f[:, q * 32:(q + 1) * 32],
                                                                   scalar1=d_sb[:, j:j + 1], scalar2=None, op0=ALU.mult),
                 reads=[r_d, R_const], writes=[r_wC])
        P.barrier()
        A.off = mark3
        Ec = A.alloc([32, CH], F32)
        Es = A.alloc([32, CH], F32)
        rhoT = A.alloc([32, CH], F32)
        io = A.alloc([CH], F32)
        mark3b = A.off
        t_a = A.alloc([4096], F32)
        t_c = A.alloc([4096], F32)
        t_i = A.alloc([4096], I32)
        r_p = P.res("s5pair")
        P.dma("sp", p_lr, lre_p[:, :], r_p, writes=[r_p])
        P.dma("sp", p_li, lim_p[:, :], r_p, writes=[r_p])
        P.dma("sp", p_dt, ldt_p[:, :], r_p, writes=[r_p])

        def dp(fn, eng="dve"):
            P.op(eng, fn, reads=[r_p], writes=[r_p])

        dp(lambda e: e.activation(out=p_dt, in_=p_dt, func=AF.Exp), "act")
        dp(lambda e: e.tensor_tensor(out=rho, in0=p_lr, in1=p_dt, op=ALU.mult))
        dp(lambda e: e.activation(out=rho, in_=rho, func=AF.Exp), "act")
        dp(lambda e: e.tensor_tensor(out=th, in0=p_li, in1=p_dt, op=ALU.mult))
        dp(lambda e: e.tensor_scalar(out=p_t, in0=th, scalar1=1.0 / TWO_PI, scalar2=None, op0=ALU.mult))
        dp(lambda e: e.tensor_copy(out=p_i, in_=p_t))
        dp(lambda e: e.tensor_copy(out=p_t, in_=p_i))
        dp(lambda e: e.scalar_tensor_tensor(out=th, in0=p_t, scalar=-CW1, in1=th, op0=ALU.mult, op1=ALU.add))
        dp(lambda e: e.scalar_tensor_tensor(out=th, in0=p_t, scalar=-CW2, in1=th, op0=ALU.mult, op1=ALU.add))
        P.op("pool", lambda e: e.iota(io, pattern=[[1, CH]], base=1, channel_multiplier=0,
                                      allow_small_or_imprecise_dtypes=True), writes=[r_p])
        for pr in range(32):
            dp(lambda e, pr=pr: e.tensor_scalar(out=Es[:, pr, :], in0=io, scalar1=th[:, pr:pr + 1], scalar2=None, op0=ALU.mult))
            dp(lambda e, pr=pr: e.tensor_scalar(out=rhoT[:, pr, :], in0=io, scalar1=0.0, scalar2=rho[:, pr:pr + 1],
                                                op0=ALU.mult, op1=ALU.add))
        Esf = Es.rearrange("p a b -> p (a b)")
        Ecf = Ec.rearrange("p a b -> p (a b)")
        for hf in range(2):
            sl = slice(hf * 4096, (hf + 1) * 4096)

            def dq(fn, eng="dve"):
                P.op(eng, fn, reads=[r_p, r_s], writes=[r_p, r_s])

            dq(lambda e, sl=sl: e.tensor_scalar(out=t_a, in0=Esf[:, sl], scalar1=1.0 / TWO_PI, scalar2=None, op0=ALU.mult))
            dq(lambda e: e.tensor_copy(out=t_i, in_=t_a))
            dq(lambda e: e.tensor_copy(out=t_a, in_=t_i))
            dq(lambda e, sl=sl: e.scalar_tensor_tensor(out=t_c, in0=t_a, scalar=-CW1, in1=Esf[:, sl], op0=ALU.mult, op1=ALU.add))
            dq(lambda e: e.scalar_tensor_tensor(out=t_c, in0=t_a, scalar=-CW2, in1=t_c, op0=ALU.mult, op1=ALU.add))
            dq(lambda e: e.tensor_scalar(out=t_c, in0=t_c, scalar1=math.pi, scalar2=-math.pi, op0=ALU.min, op1=ALU.max))
            dq(lambda e, sl=sl: e.activation(out=Esf[:, sl], in_=t_c, func=AF.Sin), "act")
            dq(lambda e: e.activation(out=t_a, in_=t_c, func=AF.Abs), "act")
            dq(lambda e, sl=sl: e.activation(out=Ecf[:, sl], in_=t_a, func=AF.Sin, scale=-1.0, bias=hpi_c), "act")
        P.barrier()
        A.off = mark3b

        uTb1 = A.alloc([S], BF16)
        r_uT1 = P.res("uTb")
        NQ = 4
        names = ["m1", "m2", "m3", "m4", "rre", "rim", "wre", "wim", "xre", "xim"]
        Bf = [{n: A.alloc([CH] if n not in ("xre", "xim", "rre", "rim") else [1], F32) for n in names} for _ in range(NQ)]
        Pb = [[A.alloc([CH], BF16) for _ in range(4)] for _ in range(NQ)]
        R_ = [{n: P.res(n) for n in names + ["xreb", "ximb"]} for _ in range(NQ)]
        car = [[A.alloc([2], F32) for _ in range(2)] for _ in range(NQ)]
        r_car = [[P.res("car") for _ in range(2)] for _ in range(NQ)]
        yt1 = A.alloc([NWT], F32)
        gt1 = A.alloc([NWT], F32)
        ysb1 = A.alloc([NWT], BF16)
        r_yt1 = P.res("yt")
        r_g = P.res("gelu")
        r_ysb1 = P.res("ysb")
        it = 0
        for j in range(8):
            P.dma("sp", uTb1, uT_s[j, :, :], r_uT1, reads=[R_uT], writes=[r_uT1])
            for q in range(NQ):
                P.op("dve", lambda e: e.memset(car[q][0], 0.0), writes=[r_car[q][0]])
            for ck in range(NCHK):
                t0 = ck * CH
                full = ck >= 7
                cp = ck % 2
                cn = 1 - cp
                c0 = 0 if full else CH - 1
                sl = slice(c0, CH)
                bk = {q: (q, q, 4 + q) for q in range(NQ)}
                def stage1(ckx):
                    tx = ckx * CH
                    for q in range(NQ):
                        pr = j * 4 + q
                        bkr, bki, _ = bk[q]
                        P.op("pe", lambda e: e.matmul(banks[bkr][:, 0:CH], lhsT=wBre[:, pr, :], rhs=uTb1[:, tx:tx + CH],
                                                      start=True, stop=True), reads=[r_wB, r_uT1], writes=[R_bank[bkr]])
                        P.op("pe", lambda e: e.matmul(banks[bki][:, CH:2 * CH], lhsT=wBim[:, pr, :], rhs=uTb1[:, tx:tx + CH],
                                                      start=True, stop=True), reads=[r_wB, r_uT1], writes=[R_bank[bki]])

                if ck == 0:
                    stage1(0)
                for q in range(NQ):
                    pr = j * 4 + q
                    bkr, bki, _ = bk[q]
                    f_, r_ = Bf[q], R_[q]
                    x0r = banks[bkr][:, 0:CH]
                    x0i = banks[bki][:, CH:2 * CH]
                    ec = Ec[:, pr, :]
                    es = Es[:, pr, :]
                    P.op("dve", lambda e: e.tensor_tensor(out=f_["m1"], in0=x0r, in1=ec, op=ALU.mult),
                         reads=[R_bank[bkr], r_p], writes=[r_["m1"]])
                    P.op("act", lambda e: e.copy(out=f_["m4"], in_=x0r), reads=[R_bank[bkr]], writes=[r_["m4"]])
                    P.op("dve", lambda e: e.tensor_tensor(out=f_["m2"], in0=x0i, in1=es, op=ALU.mult),
                         reads=[R_bank[bki], r_p], writes=[r_["m2"]])
                    P.op("dve", lambda e: e.tensor_tensor(out=f_["m3"], in0=x0i, in1=ec, op=ALU.mult),
                         reads=[R_bank[bki], r_p], writes=[r_["m3"]])
                for q in range(NQ):
                    pr = j * 4 + q
                    f_, r_ = Bf[q], R_[q]
                    ec = Ec[:, pr, :]
                    es = Es[:, pr, :]
                    P.op("pool", lambda e: e.tensor_tensor(out=f_["m4"], in0=f_["m4"], in1=es, op=ALU.mult),
                         reads=[r_["m4"], r_p], writes=[r_["m4"]])
                for q in range(NQ):
                    f_, r_ = Bf[q], R_[q]
                    sbk = bk[q][2]
                    P.op("pe", lambda e: e.matmul(banks[sbk][:, 0:CH], lhsT=ident_f[:], rhs=f_["m1"], start=True, stop=False),
                         reads=[R_const, r_["m1"]], writes=[R_bank[sbk]])
                    P.op("pe", lambda e: e.matmul(banks[sbk][:, 0:CH], lhsT=ident_f[:], rhs=f_["m2"], start=False, stop=True),
                         reads=[R_const, r_["m2"]], writes=[R_bank[sbk]])
                    P.op("pe", lambda e: e.matmul(banks[sbk][:, CH:2 * CH], lhsT=ident_f[:], rhs=f_["m3"], start=True, stop=False),
                         reads=[R_const, r_["m3"]], writes=[R_bank[sbk]])
                    P.op("pe", lambda e: e.matmul(banks[sbk][:, CH:2 * CH], lhsT=nident_f[:], rhs=f_["m4"], start=False, stop=True),
                         reads=[R_const, r_["m4"]], writes=[R_bank[sbk]])
                for q in range(NQ):
                    pr = j * 4 + q
                    f_, r_ = Bf[q], R_[q]
                    sbk = bk[q][2]
                    rt = rhoT[:, pr, :]
                    P.op("dve", lambda e: e.tensor_tensor_scan(out=f_["wre"], data0=rt, data1=banks[sbk][:, 0:CH],
                                                               initial=car[q][cp][:, 0:1], op0=ALU.mult, op1=ALU.add),
                         reads=[R_bank[sbk], r_p, r_car[q][cp]], writes=[r_["wre"]])
                    P.op("dve", lambda e: e.tensor_tensor_scan(out=f_["wim"], data0=rt, data1=banks[sbk][:, CH:2 * CH],
                                                               initial=car[q][cp][:, 1:2], op0=ALU.mult, op1=ALU.add),
                         reads=[R_bank[sbk], r_p, r_car[q][cp]], writes=[r_["wim"]])
                L1 = slice(CH - 1, CH)
                for q in range(NQ):
                    pr = j * 4 + q
                    f_, r_ = Bf[q], R_[q]
                    ecl = Ec[:, pr, CH - 1:CH]
                    esl = Es[:, pr, CH - 1:CH]
                    P.op("dve", lambda e: e.tensor_scalar(out=f_["xre"][:, 0:1], in0=f_["wim"][:, L1], scalar1=esl, scalar2=None,
                                                          op0=ALU.mult), reads=[r_["wim"], r_p], writes=[r_["xre"]])
                    P.op("dve", lambda e: e.scalar_tensor_tensor(out=car[q][cn][:, 0:1], in0=f_["wre"][:, L1], scalar=ecl,
                                                                 in1=f_["xre"][:, 0:1], op0=ALU.mult, op1=ALU.subtract),
                         reads=[r_["wre"], r_["xre"], r_p], writes=[r_car[q][cn]])
                    P.op("dve", lambda e: e.tensor_scalar(out=f_["xim"][:, 0:1], in0=f_["wre"][:, L1], scalar1=esl, scalar2=None,
                                                          op0=ALU.mult), reads=[r_["wre"], r_p], writes=[r_["xim"]])
                    P.op("dve", lambda e: e.scalar_tensor_tensor(out=car[q][cn][:, 1:2], in0=f_["wim"][:, L1], scalar=ecl,
                                                                 in1=f_["xim"][:, 0:1], op0=ALU.mult, op1=ALU.add),
                         reads=[r_["wim"], r_["xim"], r_p], writes=[r_car[q][cn]])
                    if ck == 7:
                        P.op("dve", lambda e: e.tensor_scalar(out=car[q][cn], in0=car[q][cn], scalar1=flag, scalar2=None,
                                                              op0=ALU.mult), reads=[r_car[q][cn], R_const], writes=[r_car[q][cn]])
                if ck + 1 < NCHK:
                    stage1(ck + 1)
                if not full:
                    continue
                for q in range(NQ):
                    pr = j * 4 + q
                    f_, r_ = Bf[q], R_[q]
                    ec = Ec[:, pr, :]
                    es = Es[:, pr, :]
                    pb_ = Pb[q]
                    P.op("dve", lambda e: e.tensor_tensor(out=pb_[0], in0=f_["wre"], in1=ec, op=ALU.mult),
                         reads=[r_["wre"], r_p], writes=[r_["xreb"]])
                    P.op("dve", lambda e: e.tensor_tensor(out=pb_[1], in0=f_["wim"], in1=es, op=ALU.mult),
                         reads=[r_["wim"], r_p], writes=[r_["xreb"]])
                    P.op("pool", lambda e: e.tensor_tensor(out=pb_[2], in0=f_["wim"], in1=ec, op=ALU.mult),
                         reads=[r_["wim"], r_p], writes=[r_["ximb"]])
                    P.op("pool", lambda e: e.tensor_tensor(out=pb_[3], in0=f_["wre"], in1=es, op=ALU.mult),
                         reads=[r_["wre"], r_p], writes=[r_["ximb"]])
                for q in range(NQ):
                    pr = j * 4 + q
                    f_, r_ = Bf[q], R_[q]
                    pb_ = Pb[q]
                    ybk = bk[q][2]
                    yo = banks[ybk][0:32, 0:CH]
                    P.op("pe", lambda e: e.matmul(yo, lhsT=wCre[:, pr, :], rhs=pb_[0], start=True, stop=False),
                         reads=[r_wC, r_["xreb"]], writes=[R_bank[ybk]])
                    P.op("pe", lambda e: e.matmul(yo, lhsT=wCreN[:, pr, :], rhs=pb_[1], start=False, stop=False),
                         reads=[r_wC, r_["xreb"]], writes=[R_bank[ybk]])
                    P.op("pe", lambda e: e.matmul(yo, lhsT=wCimN[:, pr, :], rhs=pb_[2], start=False, stop=False),
                         reads=[r_wC2, r_["ximb"]], writes=[R_bank[ybk]])
                    P.op("pe", lambda e: e.matmul(yo, lhsT=wCimN[:, pr, :], rhs=pb_[3], start=False, stop=False),
                         reads=[r_wC2, r_["ximb"]], writes=[R_bank[ybk]])
                    P.op("pe", lambda e: e.matmul(yo, lhsT=wD[:, pr, :], rhs=uTb1[:, t0:t0 + CH], start=False, stop=True),
                         reads=[r_wC, r_uT1], writes=[R_bank[ybk]])
                    lo = max(t0, TW0)
                    P.op("act", lambda e: e.copy(out=yt1[q * 32:(q + 1) * 32, lo - TW0:t0 + CH - TW0],
                                                 in_=banks[ybk][0:32, lo - t0:CH]), reads=[R_bank[ybk]], writes=[r_yt1])
            P.op("dve", lambda e: e.tensor_tensor(out=gt1, in0=yt1, in1=yt1, op=ALU.mult), reads=[r_yt1], writes=[r_g])
            P.op("dve", lambda e: e.tensor_scalar(out=gt1, in0=gt1, scalar1=0.044715, scalar2=1.0, op0=ALU.mult, op1=ALU.add),
                 reads=[r_g], writes=[r_g])
            P.op("dve", lambda e: e.tensor_tensor(out=gt1, in0=gt1, in1=yt1, op=ALU.mult), reads=[r_g, r_yt1], writes=[r_g])
            P.op("act", lambda e: e.activation(out=gt1, in_=gt1, func=AF.Sigmoid, scale=1.5957691216057308),
                 reads=[r_g], writes=[r_g])
            P.op("dve", lambda e: e.tensor_tensor(out=ysb1, in0=gt1, in1=yt1, op=ALU.mult), reads=[r_g, r_yt1], writes=[r_ysb1])
            P.dma("sp", ysT_s[j, :, :], ysb1, r_ysb1, reads=[r_ysb1], writes=[R_ys])

    def phase4():
        A.reset()
        SBK = [(0, 768), (768, 768), (1536, 640)]
        NMAX = 768
        ys = A.alloc([8, NMAX], BF16)
        ya = A.alloc([8, NMAX], BF16)
        mg = A.alloc([16, NMAX], BF16)
        r_ys, r_yaa, r_mg = P.res("ys"), P.res("ya"), P.res("mg")
        wk8 = [A.alloc([8, 128], BF16) for _ in range(2)]
        r_wk8 = [P.res("wk8") for _ in range(2)]
        wk8b = [A.alloc([8, 128], BF16) for _ in range(2)]
        r_wk8b = [P.res("wk8b") for _ in range(2)]
        wk16 = [A.alloc([16, 128], BF16) for _ in range(2)]
        r_wk16 = [P.res("wk16") for _ in range(2)]
        bglu = A.alloc([8], F32)
        r_bg = P.res("bglu")
        P.dma("sp", bglu, bgluT[:, :], r_bg, writes=[r_bg])
        sg = [A.alloc([512], F32) for _ in range(2)]
        r_sg = [P.res("sg") for _ in range(2)]
        ga = [A.alloc([512], F32) for _ in range(2)]
        gs_ = [A.alloc([512], F32) for _ in range(2)]
        r_ga = [P.res("ga") for _ in range(2)]
        r_gs = [P.res("gs") for _ in range(2)]
        t1 = [A.alloc([512], F32) for _ in range(2)]
        t2 = [A.alloc([512], F32) for _ in range(2)]
        r_t1 = [P.res("t1") for _ in range(2)]
        r_t2 = [P.res("t2") for _ in range(2)]
        mo = A.alloc([16, NMAX], F32)
        r_mo = P.res("mo")
        xrow = [A.alloc([D], F32) for _ in range(2)]
        r_xrow = [P.res("xrow") for _ in range(2)]
        xmid = [A.alloc([D], F32) for _ in range(2)]
        r_xmid = [P.res("xmid") for _ in range(2)]
        w_glu_v = w_glu.rearrange("(j p) c -> p j c", p=128)
        w_abr_v = w_abr.rearrange("(j p) c -> p j c", p=128)
        w_sbr_v = w_sbr.rearrange("(j p) c -> p j c", p=128)
        w_out_v = w_out.rearrange("(j p) c -> p j c", p=128)
        it = 0
        for (s0, sn_) in SBK:
            subs = []
            o = 0
            while o < sn_:
                n = min(512, sn_ - o)
                subs.append((o, n))
                o += n
            P.dma("sp", ys[:, :, 0:sn_], ysT_s[:, :, s0:s0 + sn_].rearrange("j p t -> p j t"), r_ys, reads=[R_ys], writes=[r_ys])
            P.dma("sp", ya[:, :, 0:sn_], yaT_s[:, :, s0:s0 + sn_].rearrange("j p t -> p j t"), r_yaa, reads=[R_ya], writes=[r_yaa])
            zs = mg
            for ot in range(8):
                wb = ot % 2
                P.dma("pool", wk8[wb], w_glu_v[:, :, ot * 128:(ot + 1) * 128], r_wk8[wb], writes=[r_wk8[wb]])
                for (o, n) in subs:
                    bk = it % 4
                    it += 1
                    for k in range(8):
                        P.op("pe", lambda e, bk=bk, wb=wb, k=k, o=o, n=n: e.matmul(
                            banks[bk][:, 0:n], lhsT=wk8[wb][:, k, :], rhs=ys[:, k, o:o + n], start=(k == 0), stop=(k == 7)),
                            reads=[r_wk8[wb], r_ys], writes=[R_bank[bk]])
                    P.op("act", lambda e, bk=bk, ot=ot, o=o, n=n: e.activation(
                        out=zs[:, ot, o:o + n], in_=banks[bk][:, 0:n], func=AF.Sigmoid, bias=bglu[:, ot:ot + 1]),
                        reads=[R_bank[bk], r_bg], writes=[r_mg])
            P.op("dve", lambda e, sn_=sn_: e.tensor_tensor(out=ys[:, :, 0:sn_], in0=ys[:, :, 0:sn_], in1=zs[:, 0:8, 0:sn_],
                                                           op=ALU.mult), reads=[r_ys, r_mg], writes=[r_ys])
            for dt_ in range(16):
                wb = dt_ % 2
                P.dma("pool", wk8[wb], w_abr_v[:, :, dt_ * 128:(dt_ + 1) * 128], r_wk8[wb], writes=[r_wk8[wb]])
                P.dma("pool", wk8b[wb], w_sbr_v[:, :, dt_ * 128:(dt_ + 1) * 128], r_wk8b[wb], writes=[r_wk8b[wb]])
                for (o, n) in subs:
                    b = it % 2
                    bka = (it % 2) * 2
                    bks = bka + 1
                    it += 1
                    P.dma("sp", ga[b][:, 0:n], gT_s[dt_, :, s0 + o:s0 + o + n], r_ga[b], reads=[R_gT], writes=[r_ga[b]])
                    P.dma("sp", gs_[b][:, 0:n], gT_s[16 + dt_, :, s0 + o:s0 + o + n], r_gs[b], reads=[R_gT], writes=[r_gs[b]])
                    for k in range(8):
                        P.op("pe", lambda e, bka=bka, wb=wb, k=k, o=o, n=n: e.matmul(
                            banks[bka][:, 0:n], lhsT=wk8[wb][:, k, :], rhs=ya[:, k, o:o + n], start=(k == 0), stop=(k == 7)),
                            reads=[r_wk8[wb], r_yaa], writes=[R_bank[bka]])
                    for k in range(8):
                        P.op("pe", lambda e, bks=bks, wb=wb, k=k, o=o, n=n: e.matmul(
                            banks[bks][:, 0:n], lhsT=wk8b[wb][:, k, :], rhs=ys[:, k, o:o + n], start=(k == 0), stop=(k == 7)),
                            reads=[r_wk8b[wb], r_ys], writes=[R_bank[bks]])
                    P.op("dve", lambda e, b=b, bka=bka, n=n: e.tensor_tensor(out=t1[b][:, 0:n], in0=banks[bka][:, 0:n],
                                                                            in1=ga[b][:, 0:n], op=ALU.mult),
                         reads=[R_bank[bka], r_ga[b]], writes=[r_t1[b]])
                    P.op("dve", lambda e, b=b, bks=bks, n=n: e.tensor_tensor(out=t2[b][:, 0:n], in0=banks[bks][:, 0:n],
                                                                            in1=gs_[b][:, 0:n], op=ALU.mult),
                         reads=[R_bank[bks], r_gs[b]], writes=[r_t2[b]])
                    P.op("dve", lambda e, b=b, dt_=dt_, o=o, n=n: e.tensor_tensor(out=mg[:, dt_, o:o + n], in0=t1[b][:, 0:n],
                                                                                  in1=t2[b][:, 0:n], op=ALU.add),
                         reads=[r_t1[b], r_t2[b]], writes=[r_mg])
            ntile = sn_ // 128
            for dt_ in range(16):
                wb = dt_ % 2
                P.dma("pool", wk16[wb], w_out_v[:, :, dt_ * 128:(dt_ + 1) * 128], r_wk16[wb], writes=[r_wk16[wb]])
                for (o, n) in subs:
                    bk = it % 2
                    it += 1
                    for k in range(16):
                        P.op("pe", lambda e: e.matmul(banks[bk][:, 0:n], lhsT=wk16[wb][:, k, :], rhs=mg[:, k, o:o + n],
                                                      start=(k == 0), stop=(k == 15)),
                             reads=[r_wk16[wb], r_mg], writes=[R_bank[bk]])
                    P.op("act", lambda e: e.activation(out=mo[:, dt_, o:o + n], in_=banks[bk][:, 0:n], func=AF.Copy,
                                                       scale=g1(dt_)),
                         reads=[R_bank[bk], R_mod], writes=[r_mo])
            mo2 = mo
            for tt in range(ntile):
                tg = s0 // 128 + tt
                b = tt % 2
                P.dma("sp", xrow[b], xin[(W0 + tg) * 128:(W0 + tg + 1) * 128, :], r_xrow[b], writes=[r_xrow[b]])
                for g in range(4):
                    bk = 4 + (g % 2) + 2 * (tt % 2)
                    for q in range(4):
                        dt_ = g * 4 + q
                        P.op("pe", lambda e, bk=bk, q=q, dt_=dt_: e.transpose(out=banks[bk][:, q * 128:(q + 1) * 128],
                                                                              in_=mo2[:, dt_, tt * 128:(tt + 1) * 128], identity=ident_f[:]),
                             reads=[r_mo, R_const], writes=[R_bank[bk]])
                    P.op("dve", lambda e, b=b, bk=bk, g=g: e.tensor_tensor(out=xmid[b][:, g * 512:(g + 1) * 512], in0=banks[bk][:, :],
                                                                          in1=xrow[b][:, g * 512:(g + 1) * 512], op=ALU.add),
                         reads=[R_bank[bk], r_xrow[b]], writes=[r_xmid[b]])
                P.dma("sp", xm_s[tg * 128:(tg + 1) * 128, :], xmid[b], r_xmid[b], reads=[r_xmid[b]], writes=[R_xm])

    def phase5():
        A.reset()
        h2T = A.alloc([16, NWT], BF16)
        R_h2 = [[P.res("h2T"), P.res("h2T")] for _ in range(NW)]
        mark = A.off
        first = [True]

        def src(t):
            return xm_s[t * 128:(t + 1) * 128, :]

        P.barrier()
        norm_to_hT(src, NW, h2T, R_h2, gs2, sh2, "p5")
        allh2 = [r for pr in R_h2 for r in pr]
        r_h2d = P.res("h2dma")
        R_h2s = P.res("h2s", accum=True)
        P.dma("sp", h2T_s.rearrange("j p t -> p j t"), h2T, r_h2d, reads=allh2, writes=[R_h2s])
        P.barrier()
        A.reset()
        hb1 = A.alloc([16, 512], BF16)
        hb = [hb1, hb1]
        r_hb1 = P.res("hb")
        r_hb = [r_hb1, r_hb1]
        cw = A.alloc([3, 88], F32)
        cb = A.alloc([88], F32)
        r_cw = P.res("cw")
        P.dma("sp", cw.rearrange("p a b -> p (a b)"), cwT[:, :], r_cw, writes=[r_cw])
        P.dma("sp", cb, cbT[:, :], r_cw, writes=[r_cw])
        prev2 = A.alloc([88, 2], F32)
        r_prev = [P.res("prev") for _ in range(88)]
        actT = A.alloc([NF, 512], BF16)
        r_act = [P.res("actT") for _ in range(NF)]
        wu = [[A.alloc([16, 256], BF16) for _ in range(2)] for _ in range(2)]
        r_wu = [[P.res("wu") for _ in range(2)] for _ in range(2)]
        wd = [A.alloc([NF, 128], BF16) for _ in range(2)]
        r_wd = [P.res("wd") for _ in range(2)]
        upb = [[A.alloc([514], F32) for _ in range(2)] for _ in range(2)]
        r_upb = [[P.res("upb") for _ in range(2)] for _ in range(2)]
        cv = [[A.alloc([512], F32) for _ in range(2)] for _ in range(2)]
        r_cv = [[P.res("cv") for _ in range(2)] for _ in range(2)]
        sgl = [A.alloc([512], F32) for _ in range(2)]
        r_sgl = [P.res("sgl") for _ in range(2)]
        mo = A.alloc([16, 512], F32)
        r_mo = [P.res("mo5") for _ in range(4)]
        xrow1 = A.alloc([D], F32)
        xrow = [xrow1, xrow1]
        r_xrow1 = P.res("xrow5")
        r_xrow = [r_xrow1, r_xrow1]
        xo = [A.alloc([D], F32) for _ in range(2)]
        r_xo = [P.res("xo") for _ in range(2)]
        w_up_v = w_up.rearrange("(j p) c -> p j c", p=128)
        w_dn_v = w_down.rearrange("(j p) c -> p j c", p=128)
        it = 0
        hbh = A.alloc([16, 2], BF16)
        r_hbh = P.res("hbh")
        P.dma("sp", hbh, h2T_s[:, :, 126:128].rearrange("j p t -> p j t"), r_hbh, reads=[R_h2s], writes=[r_hbh])
        R_wus = P.res("wus", accum=True)
        R_wds = P.res("wds", accum=True)
        r_wuS = [[P.res("wuS") for _ in range(2)] for _ in range(2)]
        r_wuH = [[P.res("wuH") for _ in range(2)] for _ in range(2)]
        r_wdS = [P.res("wdS") for _ in range(2)]
        r_wdH = [P.res("wdH") for _ in range(2)]
        blocks = [(128 + 512 * i, 512, False) for i in range(4)]
        for bi, (c0, n, halo) in enumerate(blocks):
            hbi = bi % 2
            P.dma("sp", hb[hbi][:, :, 0:n], h2T_s[:, :, c0:c0 + n].rearrange("j p t -> p j t"), r_hb[hbi],
                  reads=[R_h2s], writes=[r_hb[hbi]])
            rh = [r_hb[hbi]]
            for vt in range(NF):
                wb = (vt // 2) % 2
                vv = vt % 2
                if vv == 0:
                    for s_ in range(2):
                        col0 = s_ * FH + vt * 128
                        wflat = wu[wb][s_].rearrange("p a b -> p (a b)")
                        if bi == 0:
                            P.dma("pool", wu[wb][s_], w_up_v[:, :, col0:col0 + 256], r_wu[wb][s_], writes=[r_wu[wb][s_]])
                            P.dma("sp", wus[s_, vt // 2], wflat, r_wuS[wb][s_], reads=[r_wu[wb][s_]], writes=[R_wus])
                        else:
                            P.dma("sp", wflat, wus[s_, vt // 2], r_wuH[wb][s_], reads=[R_wus], writes=[r_wu[wb][s_]])
                b = it % 2
                it += 1
                for s_ in range(2):
                    ch = s_ * NF + vt
                    bk = (it % 2) * 2 + s_
                    for k in range(16):
                        P.op("pe", lambda e: e.matmul(banks[bk][:, 0:n], lhsT=wu[wb][s_][:, k, vv * 128:(vv + 1) * 128],
                                                      rhs=hb[hbi][:, k, 0:n], start=(k == 0), stop=(k == 15)),
                             reads=[r_wu[wb][s_]] + rh, writes=[R_bank[bk]])
                    if bi == 0:
                        hbk = 6 + s_
                        for k in range(16):
                            P.op("pe", lambda e: e.matmul(banks[hbk][:, 0:2], lhsT=wu[wb][s_][:, k, vv * 128:(vv + 1) * 128],
                                                          rhs=hbh[:, k, :], start=(k == 0), stop=(k == 15)),
                                 reads=[r_wu[wb][s_], r_hbh], writes=[R_bank[hbk]])
                        P.op("dve", lambda e: e.tensor_scalar(out=prev2[:, ch, :], in0=banks[hbk][:, 0:2], scalar1=flag,
                                                              scalar2=None, op0=ALU.mult),
                             reads=[R_bank[hbk], R_const], writes=[r_prev[ch]])
                    u_ = upb[b][s_]
                    P.op("dve", lambda e, u_=u_, ch=ch: e.tensor_copy(out=u_[:, 0:2], in_=prev2[:, ch, :]),
                         reads=[r_prev[ch]], writes=[r_upb[b][s_]])
                    P.op("act", lambda e, u_=u_, bk=bk: e.copy(out=u_[:, 2:514], in_=banks[bk][:, 0:512]),
                         reads=[R_bank[bk]], writes=[r_upb[b][s_]])
                    P.op("dve", lambda e, u_=u_, ch=ch: e.tensor_copy(out=prev2[:, ch, :], in_=u_[:, 512:514]),
                         reads=[r_upb[b][s_]], writes=[r_prev[ch]])
                    c_ = cv[b][s_]
                    P.op("act", lambda e, c_=c_, u_=u_, ch=ch: e.activation(out=c_, in_=u_[:, 2:514], func=AF.Identity,
                                                                           scale=cw[:, 2, ch:ch + 1], bias=cb[:, ch:ch + 1]),
                         reads=[r_upb[b][s_], r_cw], writes=[r_cv[b][s_]])
                    P.op("dve", lambda e, c_=c_, u_=u_, ch=ch: e.scalar_tensor_tensor(
                        out=c_, in0=u_[:, 1:513], scalar=cw[:, 1, ch:ch + 1], in1=c_, op0=ALU.mult, op1=ALU.add),
                        reads=[r_upb[b][s_], r_cw, r_cv[b][s_]], writes=[r_cv[b][s_]])
                    P.op("dve", lambda e, c_=c_, u_=u_, ch=ch: e.scalar_tensor_tensor(
                        out=c_, in0=u_[:, 0:512], scalar=cw[:, 0, ch:ch + 1], in1=c_, op0=ALU.mult, op1=ALU.add),
                        reads=[r_upb[b][s_], r_cw, r_cv[b][s_]], writes=[r_cv[b][s_]])
                P.op("act", lambda e, b=b: e.activation(out=sgl[b], in_=cv[b][1], func=AF.Silu),
                     reads=[r_cv[b][1]], writes=[r_sgl[b]])
                P.op("dve", lambda e, b=b, vt=vt: e.tensor_tensor(out=actT[:, vt, :], in0=sgl[b], in1=cv[b][0], op=ALU.mult),
                     reads=[r_sgl[b], r_cv[b][0]], writes=[r_act[vt]])
            for dt_ in range(16):
                wb = dt_ % 2
                wdflat = wd[wb].rearrange("p a b -> p (a b)")
                if bi == 0:
                    P.dma("pool", wd[wb], w_dn_v[:, :, dt_ * 128:(dt_ + 1) * 128], r_wd[wb], writes=[r_wd[wb]])
                    P.dma("sp", wds[dt_], wdflat, r_wdS[wb], reads=[r_wd[wb]], writes=[R_wds])
                else:
                    P.dma("sp", wdflat, wds[dt_], r_wdH[wb], reads=[R_wds], writes=[r_wd[wb]])
                bk = 4 + dt_ % 2
                for k in range(NF):
                    P.op("pe", lambda e, bk=bk, wb=wb, k=k: e.matmul(banks[bk][:, :], lhsT=wd[wb][:, k, :], rhs=actT[:, k, :],
                                                                     start=(k == 0), stop=(k == NF - 1)),
                         reads=[r_wd[wb], r_act[k]], writes=[R_bank[bk]])
                for tt in range(4):
                    P.op("act", lambda e, bk=bk, dt_=dt_, tt=tt: e.activation(
                        out=mo[:, dt_, tt * 128:(tt + 1) * 128], in_=banks[bk][:, tt * 128:(tt + 1) * 128], func=AF.Copy,
                        scale=g2(dt_)), reads=[R_bank[bk], R_mod], writes=[r_mo[tt]])
            for tt in range(4):
                tcol = (c0 + tt * 128) // 128
                b = tt % 2
                P.dma("sp", xrow[b], xm_s[tcol * 128:(tcol + 1) * 128, :], r_xrow[b], reads=[R_xm], writes=[r_xrow[b]])
                for g in range(4):
                    bk = 6 + g % 2
                    for q in range(4):
                        dt_ = g * 4 + q
                        P.op("pe", lambda e, bk=bk, q=q, dt_=dt_, tt=tt: e.transpose(
                            out=banks[bk][:, q * 128:(q + 1) * 128], in_=mo[:, dt_, tt * 128:(tt + 1) * 128], identity=ident_f[:]),
                            reads=[r_mo[tt], R_const], writes=[R_bank[bk]])
                    P.op("dve", lambda e, b=b, bk=bk, g=g: e.tensor_tensor(out=xo[b][:, g * 512:(g + 1) * 512], in0=banks[bk][:, :],
                                                                          in1=xrow[b][:, g * 512:(g + 1) * 512], op=ALU.add),
                         reads=[R_bank[bk], r_xrow[b]], writes=[r_xo[b]])
                r0 = (tcol - 1) * 128
                P.dma("sp", out[r0:r0 + 128, :], xo[b], r_xo[b], reads=[r_xo[b]], writes=[R_out])

    def dbg_dump():
        dm = nc.dram_tensor("dbg_mod", [128, 96], F32, kind="ExternalOutput").ap()
        r = P.res("dbgm")
        P.dma("sp", dm[:, :], modv[:], r, reads=[R_mod], writes=[R_out])
        dk = nc.dram_tensor("dbg_kmean", [128, H * 16], F32, kind="ExternalOutput").ap()
        r2 = P.res("dbgk")
        P.dma("sp", dk[:, :], kmean[:], r2, reads=[R_kmean], writes=[R_out])

    phases = [phase0, phase1, phase2, phase3, phase4, phase5]
    for i, ph in enumerate(phases):
        if i > STOP:
            break
        ph()
        P.barrier()
    if DEBUG:
        dbg_dump()
    P.barrier()
    P.emit()
    return nc, P


def _host_consts(half):
    ident = np.eye(128, dtype=np.float32)
    kk = np.arange(128)[:, None]
    qq = np.arange(128)[None, :]
    caus = np.where(kk <= qq, 0.0, -BIG).astype(np.float32)
    eoh = np.zeros((16, 16 * 128), np.float32)
    for n in range(16):
        eoh[n, n * 128:(n + 1) * 128] = 1.0
    pmask = np.zeros((128, NW, 16), np.float32)
    pbias = np.zeros((128, NW, 16), np.float32)
    for i in range(NW):
        cur = (W0 + i) // 2
        for n in range(16):
            valid = n < cur and (half == 1 or n >= 8)
            if not valid:
                pmask[:, i, n] = -1e30
                pbias[:, i, n] = -BIG
    invf = (np.float32(10000.0) ** (-np.arange(64, dtype=np.float32) / np.float32(64))).astype(np.float32)
    cst = np.zeros((128, 4), np.float32)
    cst[:, 0] = float(half)
    cst[:, 1] = np.concatenate([invf, invf])
    cst[:, 2] = np.concatenate([np.ones(64, np.float32), -np.ones(64, np.float32)])
    return dict(ident=ident, caus=caus, eoh=eoh, pmask=pmask.reshape(128, -1), pbias=pbias.reshape(128, -1), cst=cst)


def _fm(v, n):
    return np.ascontiguousarray(np.asarray(v, np.float32).reshape(n, 128).T)


def make_in_maps(inp):
    x = np.asarray(inp["x"], np.float32)
    c = np.asarray(inp["c"], np.float32)
    pos = np.asarray(inp["positions"], np.int32)
    L = 0
    shared = {}
    shared["w_mod"] = np.ascontiguousarray(inp["w_mod"][L], dtype=np.float32)
    shared["bmodT"] = _fm(inp["b_mod"][L], 96)
    shared["n1gT"] = _fm(inp["norm1_g"][L], 16)
    shared["n2gT"] = _fm(inp["norm2_g"][L], 16)
    shared["w_in"] = np.ascontiguousarray(inp["w_in"][L], dtype=np.float32)
    shared["qkg"] = np.ascontiguousarray(np.stack([inp["q_norm_g"][L], inp["k_norm_g"][L]], axis=1).astype(np.float32))
    lre = np.asarray(inp["ssm_lambda_re"][L], np.float32)
    lim = np.asarray(inp["ssm_lambda_im"][L], np.float32)
    ldt = np.asarray(inp["ssm_log_dt"][L], np.float32)
    ldt_gp = np.repeat(ldt[:, None], 64, axis=1)
    shared["lre_b"] = np.ascontiguousarray(np.broadcast_to(lre.reshape(1, 4096), (128, 4096)))
    shared["lim_b"] = np.ascontiguousarray(np.broadcast_to(lim.reshape(1, 4096), (128, 4096)))
    shared["ldt_b"] = np.ascontiguousarray(np.broadcast_to(ldt_gp.reshape(1, 4096), (128, 4096)))

    def pl(a):
        return np.ascontiguousarray(a.reshape(32, 2, 64).transpose(1, 2, 0).reshape(128, 32))

    shared["lre_p"] = pl(lre)
    shared["lim_p"] = pl(lim)
    shared["ldt_p"] = pl(ldt_gp)
    bre = np.asarray(inp["ssm_b_re"][L], np.float32)
    bim = np.asarray(inp["ssm_b_im"][L], np.float32)

    def bl(bm):
        o = np.zeros((128, 32, 2, 64), np.float32)
        for pr in range(32):
            for g2 in range(2):
                g = 2 * pr + g2
                g8 = g % 8
                o[g8 * 16:(g8 + 1) * 16, pr, g2, :] = bm[g].T
        return np.ascontiguousarray(o.reshape(128, 4096))

    shared["bre_l"] = bl(bre)
    shared["bim_l"] = bl(bim)
    cre = np.asarray(inp["ssm_c_re"][L], np.float32)
    cim = np.asarray(inp["ssm_c_im"][L], np.float32)

    def cbd(cm):
        o = np.zeros((2, 64, 32, 2, 16), np.float32)
        for pr in range(32):
            for g2 in range(2):
                o[g2, :, pr, g2, :] = cm[2 * pr + g2].T
        return np.ascontiguousarray(o.reshape(128, 32 * 32))

    shared["cre_bd"] = cbd(cre)
    shared["cim_bd"] = cbd(cim)
    shared["dT"] = _fm(inp["ssm_d"][L], 8)
    shared["w_glu"] = np.ascontiguousarray(inp["w_glu"][L], dtype=np.float32)
    shared["bgluT"] = _fm(inp["b_glu"][L], 8)
    shared["w_abr"] = np.ascontiguousarray(inp["w_attn_br"][L], dtype=np.float32)
    shared["w_sbr"] = np.ascontiguousarray(inp["w_ssm_br"][L], dtype=np.float32)
    shared["w_out"] = np.ascontiguousarray(inp["w_out"][L], dtype=np.float32)
    shared["w_up"] = np.ascontiguousarray(inp["w_up"][L], dtype=np.float32)
    cw = np.asarray(inp["conv_w"][L], np.float32)
    shared["cwT"] = np.ascontiguousarray(np.stack([_fm(cw[k], 88) for k in range(3)], axis=1).reshape(128, 3 * 88))
    shared["cbT"] = _fm(inp["conv_b"][L], 88)
    shared["w_down"] = np.ascontiguousarray(inp["w_down"][L], dtype=np.float32)
    in_maps = []
    for core in range(8):
        b, half = core // 2, core % 2
        m = dict(shared)
        if half == 1:
            xs, ps_ = x[b], pos[b]
        else:
            xs = np.concatenate([x[b, 2048:], x[b, :2048]], axis=0)
            ps_ = np.concatenate([pos[b, 2048:], pos[b, :2048]], axis=0)
        m["xin"] = np.ascontiguousarray(xs)
        m["posb"] = np.ascontiguousarray(np.broadcast_to(ps_.reshape(1, S), (128, S)).astype(np.int32))
        m["cT"] = _fm(c[b], 16)
        m.update(_host_consts(half))
        in_maps.append(m)
    return in_maps


_CACHE = {}


def kernel(**inputs):
    in_maps = make_in_maps(inputs)
    if "nc" not in _CACHE:
        _CACHE["nc"] = build_program()[0]
    nc = _CACHE["nc"]
    res = run_bass_kernel_spmd(nc, in_maps, core_ids=list(range(8)))
    _CACHE["last"] = res
    outp = np.zeros((4, S, D), np.float32)
    for core in range(8):
        b, half = core // 2, core % 2
        outp[b, half * 2048:(half + 1) * 2048] = res.results[core]["out"]
    return outp
```

```python
from contextlib import ExitStack
import math
import os
import numpy as np
import concourse.bass as bass
import concourse.mybir as mybir
from concourse.bass_utils import run_bass_kernel_spmd

F32 = mybir.dt.float32
BF16 = mybir.dt.bfloat16
I32 = mybir.dt.int32
ALU = mybir.AluOpType
AF = mybir.ActivationFunctionType
AX = mybir.AxisListType

D = 2048
S = 4096
NT = 32
W0 = 15
NW = 17
TW0 = W0 * 128
NWT = NW * 128
H = 8
DH = 128
FH = 5632
NF = FH // 128
EPS = 1e-6
BIG = 30000.0
TWO_PI = 2.0 * math.pi
CW1 = 6.28125
CW2 = TWO_PI - CW1
CH = 256
NCHK = S // CH

ENGS = ("pe", "act", "dve", "pool", "sp")
DEBUG = os.environ.get("MK_DEBUG", "")
STOP = int(os.environ.get("MK_STOP", "99"))


class Res:
    __slots__ = ("name", "w", "r", "accum", "sem", "dcnt", "excl")

    def __init__(self, name, accum=False):
        self.excl = False
        self.name = name
        self.w = {}
        self.r = {}
        self.accum = accum
        self.sem = None
        self.dcnt = 0


class _Rec:
    def __init__(self):
        self.call = None

    def __getattr__(self, name):
        def f(*a, **k):
            self.call = (name, a, k)
            return self
        return f


class Prog:
    def __init__(self, nc):
        self.nc = nc
        self.stack = ExitStack()
        self.ops = {e: [] for e in ENGS}
        self.cnt = {e: 0 for e in ENGS}
        self.known = {e: {} for e in ENGS}
        self.sems = {}
        self.dres = []
        self.nres = 0
        for e in ENGS:
            self.sems["c_" + e] = self.stack.enter_context(nc.semaphore("c_" + e))

    def sb(self, name, shape, dt):
        return self.stack.enter_context(self.nc.sbuf_tensor("sb_" + name, list(shape), dt))

    def ps(self, name, shape, dt):
        return self.stack.enter_context(self.nc.psum_tensor("ps_" + name, list(shape), dt))

    def res(self, name, accum=False):
        self.nres += 1
        return Res(f"{name}{self.nres}", accum)

    def _dsem(self, r):
        if r.sem is None:
            r.sem = "d_" + r.name
            self.sems[r.sem] = self.stack.enter_context(self.nc.semaphore(r.sem))
            self.dres.append(r)
        return r.sem

    def op(self, eng, fn, reads=(), writes=(), dma=None):
        xs = [r for r in reads if r.excl]
        if xs:
            writes = list(writes) + [r for r in xs if r not in writes]
            reads = [r for r in reads if not r.excl]
        waits = {}

        def merge(d):
            for k, v in d.items():
                if v > waits.get(k, 0):
                    waits[k] = v

        for r in reads:
            merge(r.w)
        for r in writes:
            if not r.accum:
                merge(r.w)
                merge(r.r)
        own = "c_" + eng
        if eng == "pe":
            waits.pop(own, None)
        kn = self.known[eng]
        wl = []
        for k, v in waits.items():
            if v > kn.get(k, 0):
                kn[k] = v
                wl.append((k, v))
        if dma is None:
            self.cnt[eng] += 1
            tok = (own, self.cnt[eng])
            inc = 1
        else:
            s = self._dsem(dma)
            dma.dcnt += 16
            tok = (s, dma.dcnt)
            inc = 16
        for r in reads:
            if tok[1] > r.r.get(tok[0], 0):
                r.r[tok[0]] = tok[1]
        for r in writes:
            if r.accum:
                if tok[1] > r.w.get(tok[0], 0):
                    r.w[tok[0]] = tok[1]
            else:
                r.w = {tok[0]: tok[1]}
                r.r = {}
        rec = _Rec()
        fn(rec)
        assert rec.call is not None
        self.ops[eng].append((wl, rec.call, tok[0], inc))

    def dma(self, eng, out, in_, sbres, reads=(), writes=(), **kw):
        self.op(eng, lambda e: e.dma_start(out=out, in_=in_, **kw), reads, writes, dma=sbres)

    def barrier(self):
        toks = {"c_" + e: self.cnt[e] for e in ENGS if self.cnt[e] > 0}
        for r in self.dres:
            toks[r.sem] = r.dcnt
        for e in ENGS:
            kn = self.known[e]
            wl = []
            for k, v in toks.items():
                if k == "c_" + e:
                    continue
                if v > kn.get(k, 0):
                    kn[k] = v
                    wl.append((k, v))
            if wl:
                self.ops[e].append((wl, None, None, 0))

    def emit(self):
        nc = self.nc
        ops = self.ops
        sems = self.sems

        def run(ename, e):
            for wl, fn, sname, inc in ops[ename]:
                for k, v in wl:
                    e.wait_ge(sems[k], v)
                if fn is not None:
                    name, a, k = fn
                    ins = getattr(e, name)(*a, **k)
                    ins.then_inc(sems[sname], inc)

        with nc.Block() as block:
            @block.tensor
            def _(e):
                run("pe", e)

            @block.scalar
            def _(e):
                run("act", e)

            @block.vector
            def _(e):
                run("dve", e)

            @block.gpsimd
            def _(e):
                run("pool", e)

            @block.sync
            def _(e):
                run("sp", e)


class Arena:
    def __init__(self, P, nbytes):
        self.t = P.sb("arena", [128, nbytes // 4], F32)
        self.cap = nbytes
        self.off = 0

    def reset(self):
        self.off = 0

    def alloc(self, shape, dt):
        n = 1
        for s_ in shape:
            n *= s_
        esz = 2 if dt == BF16 else 4
        nb = (n * esz + 63) // 64 * 64
        o = self.off
        self.off += nb
        assert self.off <= self.cap, (self.off, self.cap)
        v = self.t[:, o // 4:(o + nb) // 4]
        if dt != F32:
            v = v.bitcast(dt)
        v = v[:, 0:n]
        if len(shape) == 2:
            v = v.rearrange("p (a b) -> p a b", a=shape[0], b=shape[1])
        elif len(shape) == 3:
            v = v.rearrange("p (a b c) -> p a b c", a=shape[0], b=shape[1], c=shape[2])
        return v


def build_program():
    nc = bass.Bass("TRN2", target_bir_lowering=False)
    P = Prog(nc)

    def din(name, shape, dt=F32):
        return nc.dram_tensor(name, list(shape), dt, kind="ExternalInput").ap()

    skind = "ExternalOutput" if DEBUG else "Internal"

    def dscr(name, shape, dt):
        return nc.dram_tensor(name, list(shape), dt, kind=skind).ap()

    xin = din("xin", [S, D])
    posb = din("posb", [128, S], I32)
    cT = din("cT", [128, 16])
    w_mod = din("w_mod", [D, 6 * D])
    bmodT = din("bmodT", [128, 96])
    n1gT = din("n1gT", [128, 16])
    n2gT = din("n2gT", [128, 16])
    w_in = din("w_in", [D, 8192])
    qkg = din("qkg", [128, 2])
    cst = din("cst", [128, 4])
    ident_d = din("ident", [128, 128])
    caus_d = din("caus", [128, 128])
    eoh_d = din("eoh", [16, 16 * 128])
    pmask_d = din("pmask", [128, NW * 16])
    pbias_d = din("pbias", [128, NW * 16])
    lre_b = din("lre_b", [128, 4096])
    lim_b = din("lim_b", [128, 4096])
    ldt_b = din("ldt_b", [128, 4096])
    lre_p = din("lre_p", [128, 32])
    lim_p = din("lim_p", [128, 32])
    ldt_p = din("ldt_p", [128, 32])
    bre_l = din("bre_l", [128, 4096])
    bim_l = din("bim_l", [128, 4096])
    cre_bd = din("cre_bd", [128, 32 * 32])
    cim_bd = din("cim_bd", [128, 32 * 32])
    dT = din("dT", [128, 8])
    w_glu = din("w_glu", [1024, 1024])
    bgluT = din("bgluT", [128, 8])
    w_abr = din("w_abr", [1024, D])
    w_sbr = din("w_sbr", [1024, D])
    w_out = din("w_out", [D, D])
    w_up = din("w_up", [D, 2 * FH])
    cwT = din("cwT", [128, 3 * 88])
    cbT = din("cbT", [128, 88])
    w_down = din("w_down", [FH, D])
    out = nc.dram_tensor("out", [2048, D], F32, kind="ExternalOutput").ap()

    cos_s = dscr("cos_s", [128, S], F32)
    sin_s = dscr("sin_s", [128, S], F32)
    kT_s = dscr("kT_s", [H, 128, S], BF16)
    vT_s = dscr("vT_s", [H, 128, S], BF16)
    uT_s = dscr("uT_s", [8, 128, S], BF16)
    qT_s = dscr("qT_s", [H, 128, NWT], BF16)
    gT_s = dscr("gT_s", [32, 128, NWT], F32)
    yaT_s = dscr("yaT_s", [H, 128, NWT], BF16)
    ysT_s = dscr("ysT_s", [8, 128, NWT], BF16)
    xm_s = dscr("xm_s", [NWT, D], F32)
    h2T_s = dscr("h2T_s", [16, 128, NWT], BF16)
    wus = nc.dram_tensor("wus", [2, 22, 128, 16 * 256], BF16, kind="Internal").ap()
    wds = nc.dram_tensor("wds", [16, 128, NF * 128], BF16, kind="Internal").ap()
    R_cos, R_sin, R_kT, R_vT, R_uT, R_qT, R_gT, R_ya, R_ys, R_xm, R_out = [
        P.res(n, accum=True) for n in ("cos", "sin", "kT", "vT", "uT", "qT", "gT", "ya", "ys", "xm", "out")]

    ident_f = P.sb("ident_f", [128, 128], F32)
    ident_b = P.sb("ident_b", [128, 128], BF16)
    nident_f = P.sb("nident_f", [128, 128], F32)
    ones_b = P.sb("ones_b", [128, 128], BF16)
    caus_b = P.sb("caus_b", [128, 128], BF16)
    eoh_b = P.sb("eoh_b", [16, 16 * 128], BF16)
    cst_sb = P.sb("cst_sb", [128, 4], F32)
    qkg_sb = P.sb("qkg_sb", [128, 2], F32)
    modv = P.sb("modv", [128, 96], F32)
    gs1 = P.sb("gs1", [128, 16], F32)
    gs2 = P.sb("gs2", [128, 16], F32)
    kmean = P.sb("kmean", [128, H * 16], F32)
    kmean_b = P.sb("kmean_b", [128, H * 16], BF16)
    pmask = P.sb("pmask", [128, NW * 16], F32)
    pbias = P.sb("pbias", [128, NW * 16], F32)
    R_const = P.res("const")
    R_const2 = P.res("constb")
    R_mod = P.res("mod")
    R_kmean = P.res("kmean")
    R_kmb = P.res("kmb")

    banks = [P.ps(f"bank{i}", [128, 512], F32) for i in range(8)]
    R_bank = [P.res(f"bank{i}_") for i in range(8)]
    for r_ in R_bank:
        r_.excl = True
    A = Arena(P, 198 * 1024)

    cc_sb = P.sb("cc_sb", [128, 2], F32)
    eps_c = cc_sb[:, 0:1]
    hpi_c = cc_sb[:, 1:2]
    flag = cst_sb[:, 0:1]
    invf2 = cst_sb[:, 1:2]
    sgn = cst_sb[:, 2:3]

    def sh1(dt_):
        return modv[:, 0 * 16 + dt_: 0 * 16 + dt_ + 1]

    def g1(dt_):
        return modv[:, 2 * 16 + dt_: 2 * 16 + dt_ + 1]

    def sh2(dt_):
        return modv[:, 3 * 16 + dt_: 3 * 16 + dt_ + 1]

    def g2(dt_):
        return modv[:, 5 * 16 + dt_: 5 * 16 + dt_ + 1]

    def phase0():
        A.reset()
        P.dma("sp", ident_f[:], ident_d[:, :], R_const, writes=[R_const])
        P.dma("sp", cst_sb[:], cst[:, :], R_const, writes=[R_const])
        P.dma("sp", qkg_sb[:], qkg[:, :], R_const, writes=[R_const])
        P.dma("sp", pmask[:], pmask_d[:, :], R_const, writes=[R_const])
        P.dma("sp", pbias[:], pbias_d[:, :], R_const, writes=[R_const])
        P.dma("pool", caus_b[:], caus_d[:, :], R_const2, writes=[R_const2])
        P.dma("pool", eoh_b[:], eoh_d[:, :], R_const2, writes=[R_const2])
        P.op("dve", lambda e: e.tensor_copy(out=ident_b[:], in_=ident_f[:]), reads=[R_const], writes=[R_const])
        P.op("dve", lambda e: e.tensor_scalar(out=nident_f[:], in0=ident_f[:], scalar1=-1.0, scalar2=None, op0=ALU.mult),
             reads=[R_const], writes=[R_const])
        P.op("dve", lambda e: e.memset(ones_b[:], 1.0), writes=[R_const])
        P.op("dve", lambda e: e.memset(cc_sb[:, 0:1], EPS), writes=[R_const])
        P.op("dve", lambda e: e.memset(cc_sb[:, 1:2], math.pi / 2), writes=[R_const])
        P.op("dve", lambda e: e.memset(kmean[:], 0.0), writes=[R_kmean])

        c_sb = A.alloc([16], F32)
        sc_b = A.alloc([16], BF16)
        bm_sb = A.alloc([96], F32)
        n1g = A.alloc([16], F32)
        n2g = A.alloc([16], F32)
        r_c = P.res("c")
        P.dma("sp", c_sb, cT[:, :], r_c, writes=[r_c])
        P.dma("sp", bm_sb, bmodT[:, :], r_c, writes=[r_c])
        P.dma("sp", n1g, n1gT[:, :], r_c, writes=[r_c])
        P.dma("sp", n2g, n2gT[:, :], r_c, writes=[r_c])
        P.op("act", lambda e: e.activation(out=sc_b, in_=c_sb, func=AF.Silu), reads=[r_c], writes=[r_c])
        wm = [A.alloc([16, 768], BF16) for _ in range(2)]
        r_wm = [P.res("wm") for _ in range(2)]
        psM = banks[0]
        w_mod_v = w_mod.rearrange("(j p) c -> p j c", p=128)
        for ch in range(16):
            b = ch % 2
            P.dma("pool", wm[b], w_mod_v[:, :, ch * 768:(ch + 1) * 768], r_wm[b], writes=[r_wm[b]])
            for ctl in range(6):
                ct = ch * 6 + ctl
                for j in range(16):
                    P.op("pe", lambda e, b=b, ctl=ctl, ct=ct, j=j: e.matmul(
                        psM[:, ct:ct + 1], lhsT=wm[b][:, j, ctl * 128:(ctl + 1) * 128], rhs=sc_b[:, j:j + 1],
                        start=(j == 0), stop=(j == 15)), reads=[r_wm[b], r_c], writes=[R_bank[0]])
        P.op("dve", lambda e: e.tensor_tensor(out=modv[:], in0=psM[:, 0:96], in1=bm_sb, op=ALU.add),
             reads=[R_bank[0], r_c], writes=[R_mod])
        P.op("dve", lambda e: e.scalar_tensor_tensor(out=gs1[:], in0=modv[:, 16:32], scalar=1.0, in1=n1g,
                                                     op0=ALU.add, op1=ALU.mult), reads=[R_mod, r_c], writes=[R_mod])
        P.op("dve", lambda e: e.scalar_tensor_tensor(out=gs2[:], in0=modv[:, 64:80], scalar=1.0, in1=n2g,
                                                     op0=ALU.add, op1=ALU.mult), reads=[R_mod, r_c], writes=[R_mod])

        pos_i = A.alloc([S], I32)
        ang = A.alloc([S], F32)
        ki = A.alloc([S], I32)
        kf = A.alloc([S], F32)
        rr = A.alloc([S], F32)
        tb = A.alloc([S], F32)
        r_t = P.res("ropetmp")
        P.dma("sp", pos_i, posb[:, :], r_t, writes=[r_t])
        P.op("dve", lambda e: e.tensor_copy(out=ang, in_=pos_i), reads=[r_t], writes=[r_t])
        P.op("dve", lambda e: e.tensor_scalar(out=ang, in0=ang, scalar1=invf2, scalar2=None, op0=ALU.mult),
             reads=[r_t, R_const], writes=[r_t])
        P.op("dve", lambda e: e.tensor_scalar(out=kf, in0=ang, scalar1=1.0 / TWO_PI, scalar2=None, op0=ALU.mult),
             reads=[r_t], writes=[r_t])
        P.op("dve", lambda e: e.tensor_copy(out=ki, in_=kf), reads=[r_t], writes=[r_t])
        P.op("dve", lambda e: e.tensor_copy(out=kf, in_=ki), reads=[r_t], writes=[r_t])
        P.op("dve", lambda e: e.scalar_tensor_tensor(out=rr, in0=kf, scalar=-CW1, in1=ang, op0=ALU.mult, op1=ALU.add),
             reads=[r_t], writes=[r_t])
        P.op("dve", lambda e: e.scalar_tensor_tensor(out=rr, in0=kf, scalar=-CW2, in1=rr, op0=ALU.mult, op1=ALU.add),
             reads=[r_t], writes=[r_t])
        P.op("dve", lambda e: e.tensor_scalar(out=rr, in0=rr, scalar1=math.pi, scalar2=-math.pi, op0=ALU.min, op1=ALU.max),
             reads=[r_t], writes=[r_t])
        r_tb = P.res("tb")
        P.op("act", lambda e: e.activation(out=tb, in_=rr, func=AF.Sin), reads=[r_t], writes=[r_tb])
        P.op("dve", lambda e: e.tensor_scalar(out=tb, in0=tb, scalar1=sgn, scalar2=None, op0=ALU.mult),
             reads=[r_tb, R_const], writes=[r_tb])
        P.dma("sp", sin_s[:, :], tb, r_tb, reads=[r_tb], writes=[R_sin])
        P.op("act", lambda e: e.activation(out=kf, in_=rr, func=AF.Abs), reads=[r_t], writes=[r_t])
        r_tc = P.res("tc")
        P.op("act", lambda e: e.activation(out=ang, in_=kf, func=AF.Sin, scale=-1.0, bias=hpi_c), reads=[r_t, R_const],
             writes=[r_tc, r_t])
        P.dma("sp", cos_s[:, :], ang, r_tc, reads=[r_tc], writes=[R_cos])

    def norm_to_hT(src_rows, ntiles, hT, R_hT, gs, shf, tag):
        xt = [A.alloc([D], F32) for _ in range(2)]
        xn = [A.alloc([D], F32) for _ in range(2)]
        junk = A.alloc([D], BF16)
        ss = [A.alloc([1], F32) for _ in range(2)]
        r_xt = [P.res(tag + "xt") for _ in range(2)]
        r_xn = [P.res(tag + "xn") for _ in range(2)]
        r_junk = P.res(tag + "junk")
        r_ss = [P.res(tag + "ss") for _ in range(2)]
        def stage_a(t):
            b = t % 2
            P.dma("sp", xt[b], src_rows(t), r_xt[b], writes=[r_xt[b]])
            P.op("dve", lambda e: e.memset(ss[b], 0.0), writes=[r_ss[b]])
            P.op("act", lambda e: e.activation(out=junk, in_=xt[b], func=AF.Square, accum_out=ss[b]),
                 reads=[r_xt[b]], writes=[r_junk, r_ss[b]])
            P.op("act", lambda e: e.activation(out=ss[b], in_=ss[b], func=AF.Sqrt, scale=1.0 / D, bias=eps_c),
                 reads=[r_ss[b], R_const], writes=[r_ss[b]])
            P.op("dve", lambda e: e.reciprocal(out=ss[b], in_=ss[b]), reads=[r_ss[b]], writes=[r_ss[b]])
            P.op("act", lambda e: e.activation(out=xn[b], in_=xt[b], func=AF.Copy, scale=ss[b]),
                 reads=[r_xt[b], r_ss[b]], writes=[r_xn[b]])
            for g in range(4):
                bk = (t % 2) * 4 + g
                for q in range(4):
                    dt_ = g * 4 + q
                    P.op("pe", lambda e: e.transpose(out=banks[bk][:, q * 128:(q + 1) * 128],
                                                     in_=xn[b][:, dt_ * 128:(dt_ + 1) * 128], identity=ident_f[:]),
                         reads=[r_xn[b], R_const], writes=[R_bank[bk]])

        def stage_b(t):
            for g in range(4):
                bk = (t % 2) * 4 + g
                for q in range(4):
                    dt_ = g * 4 + q
                    o = hT[:, dt_, t * 128:(t + 1) * 128]
                    i_ = banks[bk][:, q * 128:(q + 1) * 128]
                    if g % 2 == 0:
                        P.op("act", lambda e: e.activation(out=o, in_=i_, func=AF.Identity, scale=gs[:, dt_:dt_ + 1],
                                                           bias=shf(dt_)),
                             reads=[R_bank[bk], R_mod], writes=[R_hT[t][0]])
                    else:
                        P.op("dve", lambda e: e.tensor_scalar(out=o, in0=i_, scalar1=gs[:, dt_:dt_ + 1], scalar2=shf(dt_),
                                                              op0=ALU.mult, op1=ALU.add),
                             reads=[R_bank[bk], R_mod], writes=[R_hT[t][1]])

        stage_a(0)
        for t in range(ntiles):
            if t + 1 < ntiles:
                stage_a(t + 1)
            stage_b(t)

    def phase1():
        A.reset()
        hT = A.alloc([16, S], BF16)
        R_hT = [[P.res("hT"), P.res("hT")] for _ in range(NT)]
        mark = A.off
        norm_to_hT(lambda t: xin[t * 128:(t + 1) * 128, :], NT, hT, R_hT, gs1, sh1, "p1")
        P.barrier()
        A.off = mark
        allhT = [r for pr in R_hT for r in pr]
        wt = [A.alloc([16, 256], BF16) for _ in range(2)]
        r_wt = [P.res("wt") for _ in range(2)]
        sqb = [A.alloc([512], BF16) for _ in range(2)]
        rstd = [A.alloc([512], F32) for _ in range(2)]
        qn = [A.alloc([512], F32) for _ in range(2)]
        cs = [A.alloc([512], F32) for _ in range(2)]
        sn = [A.alloc([512], F32) for _ in range(2)]
        ta = [A.alloc([512], F32) for _ in range(2)]
        tbm = [A.alloc([512], F32) for _ in range(2)]
        obq = [A.alloc([512], BF16) for _ in range(2)]
        r_obq = [P.res("obq") for _ in range(2)]
        of = [A.alloc([512], F32) for _ in range(2)]
        ob = [A.alloc([512], BF16) for _ in range(2)]
        og = [A.alloc([512], F32) for _ in range(2)]
        r_sqb = [P.res("sqb") for _ in range(2)]
        r_rstd = [P.res("rstd") for _ in range(2)]
        r_qn = [P.res("qn") for _ in range(2)]
        r_cs = [P.res("cs") for _ in range(2)]
        r_sn = [P.res("sn") for _ in range(2)]
        r_ta = [P.res("ta") for _ in range(2)]
        r_tbm = [P.res("tbm") for _ in range(2)]
        r_of = [P.res("of") for _ in range(2)]
        r_ob = [P.res("ob") for _ in range(2)]
        r_og = [P.res("og") for _ in range(2)]
        w_in_v = w_in.rearrange("(j p) c -> p j c", p=128)
        blks = []
        for c2 in range(32):
            for s_ in range(2):
                ct = c2 * 2 + s_
                if ct < 8:
                    kind, idx = "q", ct
                elif ct < 16:
                    kind, idx = "k", ct - 8
                elif ct < 24:
                    kind, idx = "v", ct - 16
                elif ct < 32:
                    kind, idx = "u", ct - 24
                else:
                    kind, idx = "g", ct - 32
                if kind in ("q", "g"):
                    blocks = [(TW0, 128)] + [(2048 + 512 * i, 512) for i in range(4)]
                else:
                    blocks = [(512 * i, 512) for i in range(8)]
                for bi_, (t0, n) in enumerate(blocks):
                    blks.append((c2, s_, kind, idx, t0, n, s_ == 0 and bi_ == 0))

        def issue_main(bd, it):
            c2, s_, kind, idx, t0, n, first = bd
            wb = c2 % 2
            bk = it % 4
            if first:
                P.dma("pool", wt[wb], w_in_v[:, :, c2 * 256:(c2 + 1) * 256], r_wt[wb], writes=[r_wt[wb]])
            tiles = range(t0 // 128, (t0 + n) // 128)
            rh = [R_hT[t][p_] for t in tiles for p_ in range(2)]
            for j in range(16):
                P.op("pe", lambda e: e.matmul(banks[bk][:, 0:n], lhsT=wt[wb][:, j, s_ * 128:(s_ + 1) * 128],
                                              rhs=hT[:, j, t0:t0 + n], start=(j == 0), stop=(j == 15)),
                     reads=[r_wt[wb]] + rh, writes=[R_bank[bk]])

        def post(bd, it, half_):
            c2, s_, kind, idx, t0, n, first = bd
            b = it % 2
            bk = it % 4
            pso = banks[bk][:, 0:n]
            if kind in ("q", "k") and half_ == "A":
                gcol = qkg_sb[:, 0:1] if kind == "q" else qkg_sb[:, 1:2]
                sbk = 4 + b
                P.op("act", lambda e: e.activation(out=sqb[b][:, 0:n], in_=pso, func=AF.Square),
                     reads=[R_bank[bk]], writes=[r_sqb[b]])
                P.op("pe", lambda e: e.matmul(banks[sbk][:, 0:n], lhsT=ones_b[:], rhs=sqb[b][:, 0:n], start=True, stop=True),
                     reads=[r_sqb[b], R_const], writes=[R_bank[sbk]])
                P.op("act", lambda e: e.activation(out=rstd[b][:, 0:n], in_=banks[sbk][:, 0:n], func=AF.Ln, scale=1.0 / DH,
                                                   bias=eps_c), reads=[R_bank[sbk], R_const], writes=[r_rstd[b]])
                P.op("act", lambda e: e.activation(out=rstd[b][:, 0:n], in_=rstd[b][:, 0:n], func=AF.Exp, scale=-0.5),
                     reads=[r_rstd[b]], writes=[r_rstd[b]])
                P.op("dve", lambda e: e.scalar_tensor_tensor(out=qn[b][:, 0:n], in0=pso, scalar=gcol, in1=rstd[b][:, 0:n],
                                                             op0=ALU.mult, op1=ALU.mult),
                     reads=[R_bank[bk], r_rstd[b], R_const], writes=[r_qn[b]])
                P.dma("sp", cs[b][:, 0:n], cos_s[:, t0:t0 + n], r_cs[b], reads=[R_cos], writes=[r_cs[b]])
                P.dma("sp", sn[b][:, 0:n], sin_s[:, t0:t0 + n], r_sn[b], reads=[R_sin], writes=[r_sn[b]])
                if half_ == "A":
                    return
            if half_ == "A":
                return
            if kind in ("q", "k"):
                P.op("pool", lambda e: e.tensor_tensor(out=ta[b][:, 0:n], in0=qn[b][:, 0:n], in1=cs[b][:, 0:n], op=ALU.mult),
                     reads=[r_qn[b], r_cs[b]], writes=[r_ta[b]])
                P.op("dve", lambda e: e.tensor_tensor(out=tbm[b][0:64, 0:n], in0=qn[b][64:128, 0:n], in1=sn[b][64:128, 0:n],
                                                      op=ALU.mult), reads=[r_qn[b], r_sn[b]], writes=[r_tbm[b]])
                P.op("dve", lambda e: e.tensor_tensor(out=tbm[b][64:128, 0:n], in0=qn[b][0:64, 0:n], in1=sn[b][0:64, 0:n],
                                                      op=ALU.mult), reads=[r_qn[b], r_sn[b]], writes=[r_tbm[b]])
                P.op("pool", lambda e: e.tensor_tensor(out=obq[b][:, 0:n], in0=ta[b][:, 0:n], in1=tbm[b][:, 0:n], op=ALU.add),
                     reads=[r_ta[b], r_tbm[b]], writes=[r_obq[b]])
                if kind == "k":
                    nb0 = t0 // 256
                    P.op("dve", lambda e: e.tensor_reduce(out=kmean[:, idx * 16 + nb0: idx * 16 + nb0 + 2],
                                                          in_=obq[b][:, 0:512].rearrange("p (a t) -> p a t", t=256),
                                                          axis=AX.X, op=ALU.add), reads=[r_obq[b]], writes=[R_kmean])
                    P.dma("sp", kT_s[idx, :, t0:t0 + n], obq[b][:, 0:n], r_obq[b], reads=[r_obq[b]], writes=[R_kT])
                else:
                    P.dma("sp", qT_s[idx, :, t0 - TW0:t0 - TW0 + n], obq[b][:, 0:n], r_obq[b], reads=[r_obq[b]], writes=[R_qT])
            elif kind in ("q", "k"):
                pass
            elif kind in ("v", "u"):
                P.op("act", lambda e: e.copy(out=ob[b][:, 0:n], in_=pso), reads=[R_bank[bk]], writes=[r_ob[b]])
                dst = vT_s if kind == "v" else uT_s
                P.dma("sp", dst[idx, :, t0:t0 + n], ob[b][:, 0:n], r_ob[b], reads=[r_ob[b]],
                      writes=[R_vT if kind == "v" else R_uT])
            else:
                P.op("act", lambda e: e.activation(out=og[b][:, 0:n], in_=pso, func=AF.Sigmoid),
                     reads=[R_bank[bk]], writes=[r_og[b]])
                P.dma("sp", gT_s[idx, :, t0 - TW0:t0 - TW0 + n], og[b][:, 0:n], r_og[b], reads=[r_og[b]], writes=[R_gT])

        nbk = len(blks)
        for it, bd in enumerate(blks):
            issue_main(bd, it)
            if it >= 1:
                post(blks[it - 1], it - 1, "A")
            if it >= 2:
                post(blks[it - 2], it - 2, "B")
        post(blks[nbk - 1], nbk - 1, "A")
        post(blks[nbk - 2], nbk - 2, "B")
        post(blks[nbk - 1], nbk - 1, "B")
        P.op("dve", lambda e: e.tensor_scalar(out=kmean_b[:], in0=kmean[:], scalar1=1.0 / 256, scalar2=None, op0=ALU.mult),
             reads=[R_kmean], writes=[R_kmb])

    def phase2():
        A.reset()
        kT = [A.alloc([S], BF16) for _ in range(2)]
        vT = [A.alloc([S], BF16) for _ in range(2)]
        qT = [A.alloc([NWT], BF16) for _ in range(2)]
        V1 = [A.alloc([NT, 128], BF16) for _ in range(2)]
        r_kT = [P.res("kT") for _ in range(2)]
        r_vT = [P.res("vT") for _ in range(2)]
        r_qT = [P.res("qT") for _ in range(2)]
        r_V1 = [P.res("V1") for _ in range(2)]
        biasq = A.alloc([NW * 16], F32)
        r_biasq = P.res("biasq")
        scm = A.alloc([16], F32)
        top8 = A.alloc([8], F32)
        r_scm = P.res("scm")
        biasT = [A.alloc([NWT], BF16) for _ in range(2)]
        r_biasT = [P.res("biasT") for _ in range(2)]
        pT = [A.alloc([512], BF16) for _ in range(3)]
        r_pT = [P.res("pT") for _ in range(3)]
        rec = [A.alloc([512], F32) for _ in range(2)]
        r_rec = [P.res("rec") for _ in range(2)]
        yaT = [A.alloc([NWT], BF16) for _ in range(2)]
        r_ya = [P.res("yaT") for _ in range(2)]
        tb16 = [banks[6][:, :].bitcast(BF16), banks[7][:, :].bitcast(BF16)]
        scale = DH ** -0.5
        groups = [(0, 1)] + [(1 + 4 * g, 4) for g in range(4)]
        pit = 0
        def pro1(h):
            b = h % 2
            P.dma("sp", kT[b], kT_s[h, :, :], r_kT[b], reads=[R_kT], writes=[r_kT[b]])
            P.dma("sp", vT[b], vT_s[h, :, :], r_vT[b], reads=[R_vT], writes=[r_vT[b]])
            P.dma("sp", qT[b], qT_s[h, :, :], r_qT[b], reads=[R_qT], writes=[r_qT[b]])
            for g in range(4):
                tbk = g % 2
                for q in range(8):
                    t = g * 8 + q
                    P.op("pe", lambda e: e.transpose(out=tb16[tbk][:, q * 128:(q + 1) * 128], in_=vT[b][:, t * 128:(t + 1) * 128],
                                                     identity=ident_b[:]),
                         reads=[r_vT[b], R_const], writes=[R_bank[6 + tbk]])
                src_ = tb16[tbk].rearrange("p (a c) -> p a c", c=128)
                dst_ = V1[b][:, g * 8:(g + 1) * 8, :]
                if g % 2 == 0:
                    P.op("dve", lambda e: e.tensor_copy(out=dst_, in_=src_), reads=[R_bank[6 + tbk]], writes=[r_V1[b]])
                else:
                    P.op("act", lambda e: e.copy(out=dst_, in_=src_), reads=[R_bank[6 + tbk]], writes=[r_V1[b]])

        def pro2(h):
            b = h % 2
            for i in range(NW):
                P.op("pe", lambda e: e.matmul(banks[7][:, 0:16], lhsT=qT[b][:, i * 128:(i + 1) * 128],
                                              rhs=kmean_b[:, h * 16:(h + 1) * 16], start=True, stop=True),
                     reads=[r_qT[b], R_kmb], writes=[R_bank[7]])
                P.op("dve", lambda e: e.tensor_tensor(out=scm, in0=banks[7][:, 0:16], in1=pmask[:, i * 16:(i + 1) * 16],
                                                      op=ALU.add), reads=[R_bank[7], R_const], writes=[r_scm])
                P.op("dve", lambda e: e.max(out=top8, in_=scm), reads=[r_scm], writes=[r_scm])
                P.op("dve", lambda e: e.tensor_scalar(out=scm, in0=scm, scalar1=top8[:, 2:3], scalar2=None, op0=ALU.is_ge),
                     reads=[r_scm], writes=[r_scm])
                P.op("dve", lambda e: e.tensor_scalar(out=scm, in0=scm, scalar1=-1.0, scalar2=BIG, op0=ALU.add, op1=ALU.mult),
                     reads=[r_scm], writes=[r_scm])
                P.op("dve", lambda e: e.tensor_tensor(out=biasq[:, i * 16:(i + 1) * 16], in0=scm,
                                                      in1=pbias[:, i * 16:(i + 1) * 16], op=ALU.add),
                     reads=[r_scm, R_const], writes=[r_biasq])

        def pro3(h):
            b = h % 2
            for g in range(5):
                i0 = g * 4
                ni = min(4, NW - i0)
                for q in range(ni):
                    i = i0 + q
                    P.op("pe", lambda e: e.transpose(out=banks[7][0:16, q * 128:(q + 1) * 128],
                                                     in_=biasq[:, i * 16:(i + 1) * 16], identity=ident_f[:]),
                         reads=[r_biasq, R_const], writes=[R_bank[7]])
                P.op("dve", lambda e: e.tensor_copy(out=biasT[b][0:16, i0 * 128:(i0 + ni) * 128],
                                                    in_=banks[7][0:16, 0:ni * 128]),
                     reads=[R_bank[7]], writes=[r_biasT[b]])

        pro1(0)
        pro2(0)
        pro3(0)
        for h in range(H):
            b = h % 2
            for gi, (i0, ni) in enumerate(groups):
                T0 = W0 + i0
                Tl = T0 + ni - 1
                NQc = ni * 128
                ob_ = gi % 2
                o_ps = banks[2 + ob_]
                s_ps = banks[4 + ob_]
                kts = list(range(Tl + 1))

                def issue_s(kt, slot):
                    c_lo = max(0, kt - T0) * 128
                    nb = kt // 2
                    b_lo = max(0, 2 * nb + 2 - T0)
                    has_bias = b_lo < ni
                    has_caus = T0 <= kt <= Tl
                    P.op("pe", lambda e: e.matmul(banks[slot][:, c_lo:NQc], lhsT=kT[b][:, kt * 128:(kt + 1) * 128],
                                                  rhs=qT[b][:, i0 * 128 + c_lo:i0 * 128 + NQc], start=True,
                                                  stop=not (has_bias or has_caus), skip_group_check=True),
                         reads=[r_kT[b], r_qT[b]], writes=[R_bank[slot]])
                    if has_bias:
                        P.op("pe", lambda e: e.matmul(banks[slot][:, b_lo * 128:NQc], lhsT=eoh_b[0:16, nb * 128:(nb + 1) * 128],
                                                      rhs=biasT[b][0:16, (i0 + b_lo) * 128:(i0 + ni) * 128], start=False,
                                                      stop=not has_caus, skip_group_check=True),
                             reads=[R_const2, r_biasT[b]], writes=[R_bank[slot]])
                    if has_caus:
                        ct = kt - T0
                        P.op("pe", lambda e: e.matmul(banks[slot][:, ct * 128:(ct + 1) * 128], lhsT=ident_b[:], rhs=caus_b[:],
                                                      start=False, stop=True, skip_group_check=True),
                             reads=[R_const, R_const2], writes=[R_bank[slot]])

                issue_s(kts[0], 0)
                for idx_, kt in enumerate(kts):
                    slot = idx_ % 2
                    if idx_ + 1 < len(kts):
                        issue_s(kts[idx_ + 1], (idx_ + 1) % 2)
                    c_lo = max(0, kt - T0) * 128
                    pb = pit % 3
                    pit += 1
                    P.op("act", lambda e: e.activation(out=pT[pb][:, c_lo:NQc], in_=banks[slot][:, c_lo:NQc], func=AF.Exp,
                                                       scale=scale),
                         reads=[R_bank[slot]], writes=[r_pT[pb]])
                    P.op("pe", lambda e: e.matmul(o_ps[:, c_lo:NQc], lhsT=V1[b][:, kt, :], rhs=pT[pb][:, c_lo:NQc],
                                                  start=(kt == 0), stop=(kt == Tl), skip_group_check=True),
                         reads=[r_pT[pb], r_V1[b]], writes=[R_bank[2 + ob_]])
                    P.op("pe", lambda e: e.matmul(s_ps[:, c_lo:NQc], lhsT=ones_b[:], rhs=pT[pb][:, c_lo:NQc],
                                                  start=(kt == 0), stop=(kt == Tl), skip_group_check=True),
                         reads=[r_pT[pb], R_const], writes=[R_bank[4 + ob_]])
                P.op("dve", lambda e: e.reciprocal(out=rec[ob_][:, 0:NQc], in_=s_ps[:, 0:NQc]),
                     reads=[R_bank[4 + ob_]], writes=[r_rec[ob_]])
                P.op("dve", lambda e: e.tensor_tensor(out=yaT[b][:, i0 * 128:(i0 + ni) * 128], in0=o_ps[:, 0:NQc],
                                                      in1=rec[ob_][:, 0:NQc], op=ALU.mult),
                     reads=[R_bank[2 + ob_], r_rec[ob_]], writes=[r_ya[b]])
                if h + 1 < H:
                    if gi == 1:
                        pro1(h + 1)
                    elif gi == 2:
                        pro2(h + 1)
                    elif gi == 3:
                        pro3(h + 1)
            P.dma("sp", yaT_s[h, :, :], yaT[b], r_ya[b], reads=[r_ya[b]], writes=[R_ya])

    def phase3():
        A.reset()
        wBre = A.alloc([32, 128], BF16)
        wBim = A.alloc([32, 128], BF16)
        wCre = A.alloc([32, 32], BF16)
        wCim = A.alloc([32, 32], BF16)
        wD = A.alloc([32, 32], BF16)
        wCreN = A.alloc([32, 32], BF16)
        wCimN = A.alloc([32, 32], BF16)
        d_sb = A.alloc([8], F32)
        p_lr = A.alloc([32], F32)
        p_li = A.alloc([32], F32)
        p_dt = A.alloc([32], F32)
        rho = A.alloc([32], F32)
        th = A.alloc([32], F32)
        p_t = A.alloc([32], F32)
        p_i = A.alloc([32], I32)
        mark3 = A.off
        t_lr = A.alloc([4096], F32)
        t_li = A.alloc([4096], F32)
        t_dt = A.alloc([4096], F32)
        t_a = A.alloc([4096], F32)
        t_b = A.alloc([4096], F32)
        t_c = A.alloc([4096], F32)
        t_d = A.alloc([4096], F32)
        t_e = A.alloc([4096], F32)
        t_i = A.alloc([4096], I32)
        r_s = P.res("s5setup")
        P.dma("sp", t_lr, lre_b[:, :], r_s, writes=[r_s])
        P.dma("sp", t_li, lim_b[:, :], r_s, writes=[r_s])
        P.dma("sp", t_dt, ldt_b[:, :], r_s, writes=[r_s])

        def dv(fn, eng="dve"):
            P.op(eng, fn, reads=[r_s], writes=[r_s])

        def sincos(theta, s_out, c_out, tmp, tmpi, red_out):
            dv(lambda e: e.tensor_scalar(out=tmp, in0=theta, scalar1=1.0 / TWO_PI, scalar2=None, op0=ALU.mult))
            dv(lambda e: e.tensor_copy(out=tmpi, in_=tmp))
            dv(lambda e: e.tensor_copy(out=tmp, in_=tmpi))
            dv(lambda e: e.scalar_tensor_tensor(out=red_out, in0=tmp, scalar=-CW1, in1=theta, op0=ALU.mult, op1=ALU.add))
            dv(lambda e: e.scalar_tensor_tensor(out=red_out, in0=tmp, scalar=-CW2, in1=red_out, op0=ALU.mult, op1=ALU.add))
            dv(lambda e: e.tensor_scalar(out=red_out, in0=red_out, scalar1=math.pi, scalar2=-math.pi, op0=ALU.min, op1=ALU.max))
            dv(lambda e: e.activation(out=s_out, in_=red_out, func=AF.Sin), "act")
            dv(lambda e: e.activation(out=tmp, in_=red_out, func=AF.Abs), "act")
            dv(lambda e: e.activation(out=c_out, in_=tmp, func=AF.Sin, scale=-1.0, bias=hpi_c), "act")

        dv(lambda e: e.activation(out=t_dt, in_=t_dt, func=AF.Exp), "act")
        dv(lambda e: e.tensor_tensor(out=t_a, in0=t_lr, in1=t_dt, op=ALU.mult))
        dv(lambda e: e.activation(out=t_a, in_=t_a, func=AF.Exp), "act")
        dv(lambda e: e.tensor_tensor(out=t_b, in0=t_li, in1=t_dt, op=ALU.mult))
        sincos(t_b, t_c, t_d, t_e, t_i, t_dt)
        dv(lambda e: e.tensor_tensor(out=t_c, in0=t_c, in1=t_a, op=ALU.mult))
        dv(lambda e: e.tensor_tensor(out=t_d, in0=t_d, in1=t_a, op=ALU.mult))
        dv(lambda e: e.tensor_scalar(out=t_d, in0=t_d, scalar1=-1.0, scalar2=None, op0=ALU.add))
        dv(lambda e: e.tensor_tensor(out=t_a, in0=t_lr, in1=t_lr, op=ALU.mult))
        dv(lambda e: e.tensor_tensor(out=t_b, in0=t_li, in1=t_li, op=ALU.mult))
        dv(lambda e: e.tensor_tensor(out=t_a, in0=t_a, in1=t_b, op=ALU.add))
        dv(lambda e: e.reciprocal(out=t_a, in_=t_a))
        dv(lambda e: e.tensor_tensor(out=t_b, in0=t_d, in1=t_lr, op=ALU.mult))
        dv(lambda e: e.tensor_tensor(out=t_e, in0=t_c, in1=t_li, op=ALU.mult))
        dv(lambda e: e.tensor_tensor(out=t_b, in0=t_b, in1=t_e, op=ALU.add))
        dv(lambda e: e.tensor_tensor(out=t_b, in0=t_b, in1=t_a, op=ALU.mult))
        dv(lambda e: e.tensor_tensor(out=t_e, in0=t_c, in1=t_lr, op=ALU.mult))
        dv(lambda e: e.tensor_tensor(out=t_dt, in0=t_d, in1=t_li, op=ALU.mult))
        dv(lambda e: e.tensor_tensor(out=t_e, in0=t_e, in1=t_dt, op=ALU.subtract))
        dv(lambda e: e.tensor_tensor(out=t_e, in0=t_e, in1=t_a, op=ALU.mult))
        P.dma("sp", t_lr, bre_l[:, :], r_s, reads=[r_s], writes=[r_s])
        P.dma("sp", t_li, bim_l[:, :], r_s, reads=[r_s], writes=[r_s])
        r_wB = P.res("wB")
        wBre_f = wBre.rearrange("p a b -> p (a b)")
        wBim_f = wBim.rearrange("p a b -> p (a b)")
        dv(lambda e: e.tensor_tensor(out=t_a, in0=t_lr, in1=t_b, op=ALU.mult))
        dv(lambda e: e.tensor_tensor(out=t_c, in0=t_li, in1=t_e, op=ALU.mult))
        P.op("dve", lambda e: e.tensor_tensor(out=wBre_f, in0=t_a, in1=t_c, op=ALU.subtract), reads=[r_s], writes=[r_wB])
        dv(lambda e: e.tensor_tensor(out=t_a, in0=t_li, in1=t_b, op=ALU.mult))
        dv(lambda e: e.tensor_tensor(out=t_c, in0=t_lr, in1=t_e, op=ALU.mult))
        P.op("dve", lambda e: e.tensor_tensor(out=wBim_f, in0=t_a, in1=t_c, op=ALU.add), reads=[r_s], writes=[r_wB])
        r_wC = P.res("wC")
        P.dma("pool", wCre.rearrange("p a b -> p (a b)"), cre_bd[:, :], r_wC, writes=[r_wC])
        r_wC2 = P.res("wC2")
        P.dma("pool", wCim.rearrange("p a b -> p (a b)"), cim_bd[:, :], r_wC2, writes=[r_wC2])
        r_d = P.res("d")
        P.dma("sp", d_sb, dT[:, :], r_d, writes=[r_d])
        P.op("dve", lambda e: e.tensor_scalar(out=wCreN.rearrange("p a b -> p (a b)"), in0=wCre.rearrange("p a b -> p (a b)"),
                                              scalar1=-1.0, scalar2=None, op0=ALU.mult), reads=[r_wC], writes=[r_wC])
        P.op("dve", lambda e: e.tensor_scalar(out=wCimN.rearrange("p a b -> p (a b)"), in0=wCim.rearrange("p a b -> p (a b)"),
                                              scalar1=-1.0, scalar2=None, op0=ALU.mult), reads=[r_wC2], writes=[r_wC2])
        for pr in range(32):
            j, q = pr // 4, pr % 4
            P.op("dve", lambda e, pr=pr, j=j, q=q: e.tensor_scalar(out=wD[:, pr, :], in0=ident_f[:, q * 32:(q + 1) * 32],
                                                                   scalar1=d_sb[:, j:j + 1], scalar2=None, op0=ALU.mult),
                 reads=[r_d, R_const], writes=[r_wC])
        P.barrier()
        A.off = mark3
        Ec = A.alloc([32, CH], F32)
        Es = A.alloc([32, CH], F32)
        rhoT = A.alloc([32, CH], F32)
        io = A.alloc([CH], F32)
        mark3b = A.off
        t_a = A.alloc([4096], F32)
        t_c = A.alloc([4096], F32)
        t_i = A.alloc([4096], I32)
        r_p = P.res("s5pair")
        P.dma("sp", p_lr, lre_p[:, :], r_p, writes=[r_p])
        P.dma("sp", p_li, lim_p[:, :], r_p, writes=[r_p])
        P.dma("sp", p_dt, ldt_p[:, :], r_p, writes=[r_p])

        def dp(fn, eng="dve"):
            P.op(eng, fn, reads=[r_p], writes=[r_p])

        dp(lambda e: e.activation(out=p_dt, in_=p_dt, func=AF.Exp), "act")
        dp(lambda e: e.tensor_tensor(out=rho, in0=p_lr, in1=p_dt, op=ALU.mult))
        dp(lambda e: e.activation(out=rho, in_=rho, func=AF.Exp), "act")
        dp(lambda e: e.tensor_tensor(out=th, in0=p_li, in1=p_dt, op=ALU.mult))
        dp(lambda e: e.tensor_scalar(out=p_t, in0=th, scalar1=1.0 / TWO_PI, scalar2=None, op0=ALU.mult))
        dp(lambda e: e.tensor_copy(out=p_i, in_=p_t))
        dp(lambda e: e.tensor_copy(out=p_t, in_=p_i))
        dp(lambda e: e.scalar_tensor_tensor(out=th, in0=p_t, scalar=-CW1, in1=th, op0=ALU.mult, op1=ALU.add))
        dp(lambda e: e.scalar_tensor_tensor(out=th, in0=p_t, scalar=-CW2, in1=th, op0=ALU.mult, op1=ALU.add))
        P.op("pool", lambda e: e.iota(io, pattern=[[1, CH]], base=1, channel_multiplier=0,
                                      allow_small_or_imprecise_dtypes=True), writes=[r_p])
        for pr in range(32):
            dp(lambda e, pr=pr: e.tensor_scalar(out=Es[:, pr, :], in0=io, scalar1=th[:, pr:pr + 1], scalar2=None, op0=ALU.mult))
            dp(lambda e, pr=pr: e.tensor_scalar(out=rhoT[:, pr, :], in0=io, scalar1=0.0, scalar2=rho[:, pr:pr + 1],
                                                op0=ALU.mult, op1=ALU.add))
        Esf = Es.rearrange("p a b -> p (a b)")
        Ecf = Ec.rearrange("p a b -> p (a b)")
        for hf in range(2):
            sl = slice(hf * 4096, (hf + 1) * 4096)

            def dq(fn, eng="dve"):
                P.op(eng, fn, reads=[r_p, r_s], writes=[r_p, r_s])

            dq(lambda e, sl=sl: e.tensor_scalar(out=t_a, in0=Esf[:, sl], scalar1=1.0 / TWO_PI, scalar2=None, op0=ALU.mult))
            dq(lambda e: e.tensor_copy(out=t_i, in_=t_a))
            dq(lambda e: e.tensor_copy(out=t_a, in_=t_i))
            dq(lambda e, sl=sl: e.scalar_tensor_tensor(out=t_c, in0=t_a, scalar=-CW1, in1=Esf[:, sl], op0=ALU.mult, op1=ALU.add))
            dq(lambda e: e.scalar_tensor_tensor(out=t_c, in0=t_a, scalar=-CW2, in1=t_c, op0=ALU.mult, op1=ALU.add))
            dq(lambda e: e.tensor_scalar(out=t_c, in0=t_c, scalar1=math.pi, scalar2=-math.pi, op0=ALU.min, op1=ALU.max))
            dq(lambda e, sl=sl: e.activation(out=Esf[:, sl], in_=t_c, func=AF.Sin), "act")
            dq(lambda e: e.activation(out=t_a, in_=t_c, func=AF.Abs), "act")
            dq(lambda e, sl=sl: e.activation(out=Ecf[:, sl], in_=t_a, func=AF.Sin, scale=-1.0, bias=hpi_c), "act")
        P.barrier()
        A.off = mark3b

        uTb1 = A.alloc([S], BF16)
        r_uT1 = P.res("uTb")
        NQ = 4
        names = ["m1", "m2", "m3", "m4", "rre", "rim", "wre", "wim", "xre", "xim"]
        Bf = [{n: A.alloc([CH] if n not in ("xre", "xim", "rre", "rim") else [1], F32) for n in names} for _ in range(NQ)]
        Pb = [[A.alloc([CH], BF16) for _ in range(4)] for _ in range(NQ)]
        R_ = [{n: P.res(n) for n in names + ["xreb", "ximb"]} for _ in range(NQ)]
        car = [[A.alloc([2], F32) for _ in range(2)] for _ in range(NQ)]
        r_car = [[P.res("car") for _ in range(2)] for _ in range(NQ)]
        yt1 = A.alloc([NWT], F32)
        gt1 = A.alloc([NWT], F32)
        ysb1 = A.alloc([NWT], BF16)
        r_yt1 = P.res("yt")
        r_g = P.res("gelu")
        r_ysb1 = P.res("ysb")
        it = 0
        for j in range(8):
            P.dma("sp", uTb1, uT_s[j, :, :], r_uT1, reads=[R_uT], writes=[r_uT1])
            for q in range(NQ):
                P.op("dve", lambda e: e.memset(car[q][0], 0.0), writes=[r_car[q][0]])
            for ck in range(NCHK):
                t0 = ck * CH
                full = ck >= 7
                cp = ck % 2
                cn = 1 - cp
                c0 = 0 if full else CH - 1
                sl = slice(c0, CH)
                bk = {q: (q, q, 4 + q) for q in range(NQ)}
                def stage1(ckx):
                    tx = ckx * CH
                    for q in range(NQ):
                        pr = j * 4 + q
                        bkr, bki, _ = bk[q]
                        P.op("pe", lambda e: e.matmul(banks[bkr][:, 0:CH], lhsT=wBre[:, pr, :], rhs=uTb1[:, tx:tx + CH],
                                                      start=True, stop=True), reads=[r_wB, r_uT1], writes=[R_bank[bkr]])
                        P.op("pe", lambda e: e.matmul(banks[bki][:, CH:2 * CH], lhsT=wBim[:, pr, :], rhs=uTb1[:, tx:tx + CH],
                                                      start=True, stop=True), reads=[r_wB, r_uT1], writes=[R_bank[bki]])

                if ck == 0:
                    stage1(0)
                for q in range(NQ):
                    pr = j * 4 + q
                    bkr, bki, _ = bk[q]
                    f_, r_ = Bf[q], R_[q]
                    x0r = banks[bkr][:, 0:CH]
                    x0i = banks[bki][:, CH:2 * CH]
                    ec = Ec[:, pr, :]
                    es = Es[:, pr, :]
                    P.op("dve", lambda e: e.tensor_tensor(out=f_["m1"], in0=x0r, in1=ec, op=ALU.mult),
                         reads=[R_bank[bkr], r_p], writes=[r_["m1"]])
                    P.op("act", lambda e: e.copy(out=f_["m4"], in_=x0r), reads=[R_bank[bkr]], writes=[r_["m4"]])
                    P.op("dve", lambda e: e.tensor_tensor(out=f_["m2"], in0=x0i, in1=es, op=ALU.mult),
                         reads=[R_bank[bki], r_p], writes=[r_["m2"]])
                    P.op("dve", lambda e: e.tensor_tensor(out=f_["m3"], in0=x0i, in1=ec, op=ALU.mult),
                         reads=[R_bank[bki], r_p], writes=[r_["m3"]])
                for q in range(NQ):
                    pr = j * 4 + q
                    f_, r_ = Bf[q], R_[q]
                    ec = Ec[:, pr, :]
                    es = Es[:, pr, :]
                    P.op("pool", lambda e: e.tensor_tensor(out=f_["m4"], in0=f_["m4"], in1=es, op=ALU.mult),
                         reads=[r_["m4"], r_p], writes=[r_["m4"]])
                for q in range(NQ):
                    f_, r_ = Bf[q], R_[q]
                    sbk = bk[q][2]
                    P.op("pe", lambda e: e.matmul(banks[sbk][:, 0:CH], lhsT=ident_f[:], rhs=f_["m1"], start=True, stop=False),
                         reads=[R_const, r_["m1"]], writes=[R_bank[sbk]])
                    P.op("pe", lambda e: e.matmul(banks[sbk][:, 0:CH], lhsT=ident_f[:], rhs=f_["m2"], start=False, stop=True),
                         reads=[R_const, r_["m2"]], writes=[R_bank[sbk]])
                    P.op("pe", lambda e: e.matmul(banks[sbk][:, CH:2 * CH], lhsT=ident_f[:], rhs=f_["m3"], start=True, stop=False),
                         reads=[R_const, r_["m3"]], writes=[R_bank[sbk]])
                    P.op("pe", lambda e: e.matmul(banks[sbk][:, CH:2 * CH], lhsT=nident_f[:], rhs=f_["m4"], start=False, stop=True),
                         reads=[R_const, r_["m4"]], writes=[R_bank[sbk]])
                for q in range(NQ):
                    pr = j * 4 + q
                    f_, r_ = Bf[q], R_[q]
                    sbk = bk[q][2]
                    rt = rhoT[:, pr, :]
                    P.op("dve", lambda e: e.tensor_tensor_scan(out=f_["wre"], data0=rt, data1=banks[sbk][:, 0:CH],
                                                               initial=car[q][cp][:, 0:1], op0=ALU.mult, op1=ALU.add),
                         reads=[R_bank[sbk], r_p, r_car[q][cp]], writes=[r_["wre"]])
                    P.op("dve", lambda e: e.tensor_tensor_scan(out=f_["wim"], data0=rt, data1=banks[sbk][:, CH:2 * CH],
                                                               initial=car[q][cp][:, 1:2], op0=ALU.mult, op1=ALU.add),
                         reads=[R_bank[sbk], r_p, r_car[q][cp]], writes=[r_["wim"]])
                L1 = slice(CH - 1, CH)
                for q in range(NQ):
                    pr = j * 4 + q
                    f_, r_ = Bf[q], R_[q]
                    ecl = Ec[:, pr, CH - 1:CH]
                    esl = Es[:, pr, CH - 1:CH]
                    P.op("dve", lambda e: e.tensor_scalar(out=f_["xre"][:, 0:1], in0=f_["wim"][:, L1], scalar1=esl, scalar2=None,
                                                          op0=ALU.mult), reads=[r_["wim"], r_p], writes=[r_["xre"]])
                    P.op("dve", lambda e: e.scalar_tensor_tensor(out=car[q][cn][:, 0:1], in0=f_["wre"][:, L1], scalar=ecl,
                                                                 in1=f_["xre"][:, 0:1], op0=ALU.mult, op1=ALU.subtract),
                         reads=[r_["wre"], r_["xre"], r_p], writes=[r_car[q][cn]])
                    P.op("dve", lambda e: e.tensor_scalar(out=f_["xim"][:, 0:1], in0=f_["wre"][:, L1], scalar1=esl, scalar2=None,
                                                          op0=ALU.mult), reads=[r_["wre"], r_p], writes=[r_["xim"]])
                    P.op("dve", lambda e: e.scalar_tensor_tensor(out=car[q][cn][:, 1:2], in0=f_["wim"][:, L1], scalar=ecl,
                                                                 in1=f_["xim"][:, 0:1], op0=ALU.mult, op1=ALU.add),
                         reads=[r_["wim"], r_["xim"], r_p], writes=[r_car[q][cn]])
                    if ck == 7:
                        P.op("dve", lambda e: e.tensor_scalar(out=car[q][cn], in0=car[q][cn], scalar1=flag, scalar2=None,
                                                              op0=ALU.mult), reads=[r_car[q][cn], R_const], writes=[r_car[q][cn]])
                if ck + 1 < NCHK:
                    stage1(ck + 1)
                if not full:
                    continue
                for q in range(NQ):
                    pr = j * 4 + q
                    f_, r_ = Bf[q], R_[q]
                    ec = Ec[:, pr, :]
                    es = Es[:, pr, :]
                    pb_ = Pb[q]
                    P.op("dve", lambda e: e.tensor_tensor(out=pb_[0], in0=f_["wre"], in1=ec, op=ALU.mult),
                         reads=[r_["wre"], r_p], writes=[r_["xreb"]])
                    P.op("dve", lambda e: e.tensor_tensor(out=pb_[1], in0=f_["wim"], in1=es, op=ALU.mult),
                         reads=[r_["wim"], r_p], writes=[r_["xreb"]])
                    P.op("pool", lambda e: e.tensor_tensor(out=pb_[2], in0=f_["wim"], in1=ec, op=ALU.mult),
                         reads=[r_["wim"], r_p], writes=[r_["ximb"]])
                    P.op("pool", lambda e: e.tensor_tensor(out=pb_[3], in0=f_["wre"], in1=es, op=ALU.mult),
                         reads=[r_["wre"], r_p], writes=[r_["ximb"]])
                for q in range(NQ):
                    pr = j * 4 + q
                    f_, r_ = Bf[q], R_[q]
                    pb_ = Pb[q]
                    ybk = bk[q][2]
                    yo = banks[ybk][0:32, 0:CH]
                    P.op("pe", lambda e: e.matmul(yo, lhsT=wCre[:, pr, :], rhs=pb_[0], start=True, stop=False),
                         reads=[r_wC, r_["xreb"]], writes=[R_bank[ybk]])
                    P.op("pe", lambda e: e.matmul(yo, lhsT=wCreN[:, pr, :], rhs=pb_[1], start=False, stop=False),
                         reads=[r_wC, r_["xreb"]], writes=[R_bank[ybk]])
                    P.op("pe", lambda e: e.matmul(yo, lhsT=wCimN[:, pr, :], rhs=pb_[2], start=False, stop=False),
                         reads=[r_wC2, r_["ximb"]], writes=[R_bank[ybk]])
                    P.op("pe", lambda e: e.matmul(yo, lhsT=wCimN[:, pr, :], rhs=pb_[3], start=False, stop=False),
                         reads=[r_wC2, r_["ximb"]], writes=[R_bank[ybk]])
                    P.op("pe", lambda e: e.matmul(yo, lhsT=wD[:, pr, :], rhs=uTb1[:, t0:t0 + CH], start=False, stop=True),
                         reads=[r_wC, r_uT1], writes=[R_bank[ybk]])
                    lo = max(t0, TW0)
                    P.op("act", lambda e: e.copy(out=yt1[q * 32:(q + 1) * 32, lo - TW0:t0 + CH - TW0],
                                                 in_=banks[ybk][0:32, lo - t0:CH]), reads=[R_bank[ybk]], writes=[r_yt1])
            P.op("dve", lambda e: e.tensor_tensor(out=gt1, in0=yt1, in1=yt1, op=ALU.mult), reads=[r_yt1], writes=[r_g])
            P.op("dve", lambda e: e.tensor_scalar(out=gt1, in0=gt1, scalar1=0.044715, scalar2=1.0, op0=ALU.mult, op1=ALU.add),
                 reads=[r_g], writes=[r_g])
            P.op("dve", lambda e: e.tensor_tensor(out=gt1, in0=gt1, in1=yt1, op=ALU.mult), reads=[r_g, r_yt1], writes=[r_g])
            P.op("act", lambda e: e.activation(out=gt1, in_=gt1, func=AF.Sigmoid, scale=1.5957691216057308),
                 reads=[r_g], writes=[r_g])
            P.op("dve", lambda e: e.tensor_tensor(out=ysb1, in0=gt1, in1=yt1, op=ALU.mult), reads=[r_g, r_yt1], writes=[r_ysb1])
            P.dma("sp", ysT_s[j, :, :], ysb1, r_ysb1, reads=[r_ysb1], writes=[R_ys])

    def phase4():
        A.reset()
        SBK = [(0, 768), (768, 768), (1536, 640)]
        NMAX = 768
        ys = A.alloc([8, NMAX], BF16)
        ya = A.alloc([8, NMAX], BF16)
        mg = A.alloc([16, NMAX], BF16)
        r_ys, r_yaa, r_mg = P.res("ys"), P.res("ya"), P.res("mg")
        wk8 = [A.alloc([8, 128], BF16) for _ in range(4)]
        r_wk8 = [P.res("wk8") for _ in range(4)]
        wk8b = [A.alloc([8, 128], BF16) for _ in range(4)]
        r_wk8b = [P.res("wk8b") for _ in range(4)]
        wk16 = [A.alloc([16, 128], BF16) for _ in range(4)]
        r_wk16 = [P.res("wk16") for _ in range(4)]
        bglu = A.alloc([8], F32)
        r_bg = P.res("bglu")
        P.dma("sp", bglu, bgluT[:, :], r_bg, writes=[r_bg])
        sg = [A.alloc([512], F32) for _ in range(2)]
        r_sg = [P.res("sg") for _ in range(2)]
        ga = [A.alloc([512], F32) for _ in range(2)]
        gs_ = [A.alloc([512], F32) for _ in range(2)]
        r_ga = [P.res("ga") for _ in range(2)]
        r_gs = [P.res("gs") for _ in range(2)]
        t1 = [A.alloc([512], F32) for _ in range(2)]
        t2 = [A.alloc([512], F32) for _ in range(2)]
        r_t1 = [P.res("t1") for _ in range(2)]
        r_t2 = [P.res("t2") for _ in range(2)]
        mo = A.alloc([16, NMAX], F32)
        r_mo = P.res("mo")
        xrow = [A.alloc([D], F32) for _ in range(2)]
        r_xrow = [P.res("xrow") for _ in range(2)]
        xmid = [A.alloc([D], F32) for _ in range(2)]
        r_xmid = [P.res("xmid") for _ in range(2)]
        w_glu_v = w_glu.rearrange("(j p) c -> p j c", p=128)
        w_abr_v = w_abr.rearrange("(j p) c -> p j c", p=128)
        w_sbr_v = w_sbr.rearrange("(j p) c -> p j c", p=128)
        w_out_v = w_out.rearrange("(j p) c -> p j c", p=128)
        it = 0
        for (s0, sn_) in SBK:
            subs = []
            o = 0
            while o < sn_:
                n = min(512, sn_ - o)
                subs.append((o, n))
                o += n
            P.dma("sp", ys[:, :, 0:sn_], ysT_s[:, :, s0:s0 + sn_].rearrange("j p t -> p j t"), r_ys, reads=[R_ys], writes=[r_ys])
            P.dma("sp", ya[:, :, 0:sn_], yaT_s[:, :, s0:s0 + sn_].rearrange("j p t -> p j t"), r_yaa, reads=[R_ya], writes=[r_yaa])
            zs = mg
            for ot in range(8):
                wb = ot % 4
                P.dma("pool", wk8[wb], w_glu_v[:, :, ot * 128:(ot + 1) * 128], r_wk8[wb], writes=[r_wk8[wb]])
                for (o, n) in subs:
                    bk = it % 4
                    it += 1
                    for k in range(8):
                        P.op("pe", lambda e, bk=bk, wb=wb, k=k, o=o, n=n: e.matmul(
                            banks[bk][:, 0:n], lhsT=wk8[wb][:, k, :], rhs=ys[:, k, o:o + n], start=(k == 0), stop=(k == 7)),
                            reads=[r_wk8[wb], r_ys], writes=[R_bank[bk]])
                    P.op("act", lambda e, bk=bk, ot=ot, o=o, n=n: e.activation(
                        out=zs[:, ot, o:o + n], in_=banks[bk][:, 0:n], func=AF.Sigmoid, bias=bglu[:, ot:ot + 1]),
                        reads=[R_bank[bk], r_bg], writes=[r_mg])
            P.op("dve", lambda e, sn_=sn_: e.tensor_tensor(out=ys[:, :, 0:sn_], in0=ys[:, :, 0:sn_], in1=zs[:, 0:8, 0:sn_],
                                                           op=ALU.mult), reads=[r_ys, r_mg], writes=[r_ys])
            for dt_ in range(16):
                wb = dt_ % 4
                P.dma("pool", wk8[wb], w_abr_v[:, :, dt_ * 128:(dt_ + 1) * 128], r_wk8[wb], writes=[r_wk8[wb]])
                P.dma("pool", wk8b[wb], w_sbr_v[:, :, dt_ * 128:(dt_ + 1) * 128], r_wk8b[wb], writes=[r_wk8b[wb]])
                for (o, n) in subs:
                    b = it % 2
                    bka = (it % 2) * 2
                    bks = bka + 1
                    it += 1
                    P.dma("sp", ga[b][:, 0:n], gT_s[dt_, :, s0 + o:s0 + o + n], r_ga[b], reads=[R_gT], writes=[r_ga[b]])
                    P.dma("sp", gs_[b][:, 0:n], gT_s[16 + dt_, :, s0 + o:s0 + o + n], r_gs[b], reads=[R_gT], writes=[r_gs[b]])
                    for k in range(8):
                        P.op("pe", lambda e, bka=bka, wb=wb, k=k, o=o, n=n: e.matmul(
                            banks[bka][:, 0:n], lhsT=wk8[wb][:, k, :], rhs=ya[:, k, o:o + n], start=(k == 0), stop=(k == 7)),
                            reads=[r_wk8[wb], r_yaa], writes=[R_bank[bka]])
                    for k in range(8):
                        P.op("pe", lambda e, bks=bks, wb=wb, k=k, o=o, n=n: e.matmul(
                            banks[bks][:, 0:n], lhsT=wk8b[wb][:, k, :], rhs=ys[:, k, o:o + n], start=(k == 0), stop=(k == 7)),
                            reads=[r_wk8b[wb], r_ys], writes=[R_bank[bks]])
                    P.op("dve", lambda e, b=b, bka=bka, n=n: e.tensor_tensor(out=t1[b][:, 0:n], in0=banks[bka][:, 0:n],
                                                                            in1=ga[b][:, 0:n], op=ALU.mult),
                         reads=[R_bank[bka], r_ga[b]], writes=[r_t1[b]])
                    P.op("dve", lambda e, b=b, bks=bks, n=n: e.tensor_tensor(out=t2[b][:, 0:n], in0=banks[bks][:, 0:n],
                                                                            in1=gs_[b][:, 0:n], op=ALU.mult),
                         reads=[R_bank[bks], r_gs[b]], writes=[r_t2[b]])
                    P.op("dve", lambda e, b=b, dt_=dt_, o=o, n=n: e.tensor_tensor(out=mg[:, dt_, o:o + n], in0=t1[b][:, 0:n],
                                                                                  in1=t2[b][:, 0:n], op=ALU.add),
                         reads=[r_t1[b], r_t2[b]], writes=[r_mg])
            ntile = sn_ // 128
            for dt_ in range(16):
                wb = dt_ % 4
                P.dma("pool", wk16[wb], w_out_v[:, :, dt_ * 128:(dt_ + 1) * 128], r_wk16[wb], writes=[r_wk16[wb]])
                for (o, n) in subs:
                    bk = it % 2
                    it += 1
                    for k in range(16):
                        P.op("pe", lambda e: e.matmul(banks[bk][:, 0:n], lhsT=wk16[wb][:, k, :], rhs=mg[:, k, o:o + n],
                                                      start=(k == 0), stop=(k == 15)),
                             reads=[r_wk16[wb], r_mg], writes=[R_bank[bk]])
                    P.op("act", lambda e: e.activation(out=mo[:, dt_, o:o + n], in_=banks[bk][:, 0:n], func=AF.Copy,
                                                       scale=g1(dt_)),
                         reads=[R_bank[bk], R_mod], writes=[r_mo])
            mo2 = mo
            for tt in range(ntile):
                tg = s0 // 128 + tt
                b = tt % 2
                P.dma("sp", xrow[b], xin[(W0 + tg) * 128:(W0 + tg + 1) * 128, :], r_xrow[b], writes=[r_xrow[b]])
                for g in range(4):
                    bk = 4 + (g % 2) + 2 * (tt % 2)
                    for q in range(4):
                        dt_ = g * 4 + q
                        P.op("pe", lambda e, bk=bk, q=q, dt_=dt_: e.transpose(out=banks[bk][:, q * 128:(q + 1) * 128],
                                                                              in_=mo2[:, dt_, tt * 128:(tt + 1) * 128], identity=ident_f[:]),
                             reads=[r_mo, R_const], writes=[R_bank[bk]])
                    P.op("dve", lambda e, b=b, bk=bk, g=g: e.tensor_tensor(out=xmid[b][:, g * 512:(g + 1) * 512], in0=banks[bk][:, :],
                                                                          in1=xrow[b][:, g * 512:(g + 1) * 512], op=ALU.add),
                         reads=[R_bank[bk], r_xrow[b]], writes=[r_xmid[b]])
                P.dma("sp", xm_s[tg * 128:(tg + 1) * 128, :], xmid[b], r_xmid[b], reads=[r_xmid[b]], writes=[R_xm])

    def phase5():
        A.reset()
        h2T = A.alloc([16, NWT], BF16)
        R_h2 = [[P.res("h2T"), P.res("h2T")] for _ in range(NW)]
        mark = A.off
        first = [True]

        def src(t):
            return xm_s[t * 128:(t + 1) * 128, :]

        P.barrier()
        norm_to_hT(src, NW, h2T, R_h2, gs2, sh2, "p5")
        allh2 = [r for pr in R_h2 for r in pr]
        r_h2d = P.res("h2dma")
        R_h2s = P.res("h2s", accum=True)
        P.dma("sp", h2T_s.rearrange("j p t -> p j t"), h2T, r_h2d, reads=allh2, writes=[R_h2s])
        P.barrier()
        A.reset()
        hb1 = A.alloc([16, 512], BF16)
        hb = [hb1, hb1]
        r_hb1 = P.res("hb")
        r_hb = [r_hb1, r_hb1]
        cw = A.alloc([3, 88], F32)
        cb = A.alloc([88], F32)
        r_cw = P.res("cw")
        P.dma("sp", cw.rearrange("p a b -> p (a b)"), cwT[:, :], r_cw, writes=[r_cw])
        P.dma("sp", cb, cbT[:, :], r_cw, writes=[r_cw])
        prev2 = A.alloc([88, 2], F32)
        r_prev = [P.res("prev") for _ in range(88)]
        actT = A.alloc([NF, 512], BF16)
        r_act = [P.res("actT") for _ in range(NF)]
        wu = [[A.alloc([16, 256], BF16) for _ in range(2)] for _ in range(2)]
        r_wu = [[P.res("wu") for _ in range(2)] for _ in range(2)]
        wd = [A.alloc([NF, 128], BF16) for _ in range(2)]
        r_wd = [P.res("wd") for _ in range(2)]
        upb = [[A.alloc([514], F32) for _ in range(2)] for _ in range(2)]
        r_upb = [[P.res("upb") for _ in range(2)] for _ in range(2)]
        cv = [[A.alloc([512], F32) for _ in range(2)] for _ in range(2)]
        r_cv = [[P.res("cv") for _ in range(2)] for _ in range(2)]
        sgl = [A.alloc([512], F32) for _ in range(2)]
        r_sgl = [P.res("sgl") for _ in range(2)]
        mo = A.alloc([16, 512], F32)
        r_mo = [P.res("mo5") for _ in range(4)]
        xrow1 = A.alloc([D], F32)
        xrow = [xrow1, xrow1]
        r_xrow1 = P.res("xrow5")
        r_xrow = [r_xrow1, r_xrow1]
        xo = [A.alloc([D], F32) for _ in range(2)]
        r_xo = [P.res("xo") for _ in range(2)]
        w_up_v = w_up.rearrange("(j p) c -> p j c", p=128)
        w_dn_v = w_down.rearrange("(j p) c -> p j c", p=128)
        it = 0
        hbh = A.alloc([16, 2], BF16)
        r_hbh = P.res("hbh")
        P.dma("sp", hbh, h2T_s[:, :, 126:128].rearrange("j p t -> p j t"), r_hbh, reads=[R_h2s], writes=[r_hbh])
        R_wus = P.res("wus", accum=True)
        R_wds = P.res("wds", accum=True)
        r_wuS = [[P.res("wuS") for _ in range(2)] for _ in range(2)]
        r_wuH = [[P.res("wuH") for _ in range(2)] for _ in range(2)]
        r_wdS = [P.res("wdS") for _ in range(2)]
        r_wdH = [P.res("wdH") for _ in range(2)]
        blocks = [(128 + 512 * i, 512, False) for i in range(4)]
        for bi, (c0, n, halo) in enumerate(blocks):
            hbi = bi % 2
            P.dma("sp", hb[hbi][:, :, 0:n], h2T_s[:, :, c0:c0 + n].rearrange("j p t -> p j t"), r_hb[hbi],
                  reads=[R_h2s], writes=[r_hb[hbi]])
            rh = [r_hb[hbi]]
            for vt in range(NF):
                wb = (vt // 2) % 2
                vv = vt % 2
                if vv == 0:
                    for s_ in range(2):
                        col0 = s_ * FH + vt * 128
                        wflat = wu[wb][s_].rearrange("p a b -> p (a b)")
                        if bi == 0:
                            P.dma("pool", wu[wb][s_], w_up_v[:, :, col0:col0 + 256], r_wu[wb][s_], writes=[r_wu[wb][s_]])
                            P.dma("sp", wus[s_, vt // 2], wflat, r_wuS[wb][s_], reads=[r_wu[wb][s_]], writes=[R_wus])
                        else:
                            P.dma("sp", wflat, wus[s_, vt // 2], r_wuH[wb][s_], reads=[R_wus], writes=[r_wu[wb][s_]])
                b = it % 2
                it += 1
                for s_ in range(2):
                    ch = s_ * NF + vt
                    bk = (it % 2) * 2 + s_
                    for k in range(16):
                        P.op("pe", lambda e: e.matmul(banks[bk][:, 0:n], lhsT=wu[wb][s_][:, k, vv * 128:(vv + 1) * 128],
                                                      rhs=hb[hbi][:, k, 0:n], start=(k == 0), stop=(k == 15)),
                             reads=[r_wu[wb][s_]] + rh, writes=[R_bank[bk]])
                    if bi == 0:
                        hbk = 6 + s_
                        for k in range(16):
                            P.op("pe", lambda e: e.matmul(banks[hbk][:, 0:2], lhsT=wu[wb][s_][:, k, vv * 128:(vv + 1) * 128],
                                                          rhs=hbh[:, k, :], start=(k == 0), stop=(k == 15)),
                                 reads=[r_wu[wb][s_], r_hbh], writes=[R_bank[hbk]])
                        P.op("dve", lambda e: e.tensor_scalar(out=prev2[:, ch, :], in0=banks[hbk][:, 0:2], scalar1=flag,
                                                              scalar2=None, op0=ALU.mult),
                             reads=[R_bank[hbk], R_const], writes=[r_prev[ch]])
                    u_ = upb[b][s_]
                    P.op("dve", lambda e, u_=u_, ch=ch: e.tensor_copy(out=u_[:, 0:2], in_=prev2[:, ch, :]),
                         reads=[r_prev[ch]], writes=[r_upb[b][s_]])
                    P.op("act", lambda e, u_=u_, bk=bk: e.copy(out=u_[:, 2:514], in_=banks[bk][:, 0:512]),
                         reads=[R_bank[bk]], writes=[r_upb[b][s_]])
                    P.op("dve", lambda e, u_=u_, ch=ch: e.tensor_copy(out=prev2[:, ch, :], in_=u_[:, 512:514]),
                         reads=[r_upb[b][s_]], writes=[r_prev[ch]])
                    c_ = cv[b][s_]
                    P.op("act", lambda e, c_=c_, u_=u_, ch=ch: e.activation(out=c_, in_=u_[:, 2:514], func=AF.Identity,
                                                                           scale=cw[:, 2, ch:ch + 1], bias=cb[:, ch:ch + 1]),
                         reads=[r_upb[b][s_], r_cw], writes=[r_cv[b][s_]])
                    P.op("dve", lambda e, c_=c_, u_=u_, ch=ch: e.scalar_tensor_tensor(
                        out=c_, in0=u_[:, 1:513], scalar=cw[:, 1, ch:ch + 1], in1=c_, op0=ALU.mult, op1=ALU.add),
                        reads=[r_upb[b][s_], r_cw, r_cv[b][s_]], writes=[r_cv[b][s_]])
                    P.op("dve", lambda e, c_=c_, u_=u_, ch=ch: e.scalar_tensor_tensor(
                        out=c_, in0=u_[:, 0:512], scalar=cw[:, 0, ch:ch + 1], in1=c_, op0=ALU.mult, op1=ALU.add),
                        reads=[r_upb[b][s_], r_cw, r_cv[b][s_]], writes=[r_cv[b][s_]])
                P.op("act", lambda e, b=b: e.activation(out=sgl[b], in_=cv[b][1], func=AF.Silu),
                     reads=[r_cv[b][1]], writes=[r_sgl[b]])
                P.op("dve", lambda e, b=b, vt=vt: e.tensor_tensor(out=actT[:, vt, :], in0=sgl[b], in1=cv[b][0], op=ALU.mult),
                     reads=[r_sgl[b], r_cv[b][0]], writes=[r_act[vt]])
            for dt_ in range(16):
                wb = dt_ % 2
                wdflat = wd[wb].rearrange("p a b -> p (a b)")
                if bi == 0:
                    P.dma("pool", wd[wb], w_dn_v[:, :, dt_ * 128:(dt_ + 1) * 128], r_wd[wb], writes=[r_wd[wb]])
                    P.dma("sp", wds[dt_], wdflat, r_wdS[wb], reads=[r_wd[wb]], writes=[R_wds])
                else:
                    P.dma("sp", wdflat, wds[dt_], r_wdH[wb], reads=[R_wds], writes=[r_wd[wb]])
                bk = 4 + dt_ % 2
                for k in range(NF):
                    P.op("pe", lambda e, bk=bk, wb=wb, k=k: e.matmul(banks[bk][:, :], lhsT=wd[wb][:, k, :], rhs=actT[:, k, :],
                                                                     start=(k == 0), stop=(k == NF - 1)),
                         reads=[r_wd[wb], r_act[k]], writes=[R_bank[bk]])
                for tt in range(4):
                    P.op("act", lambda e, bk=bk, dt_=dt_, tt=tt: e.activation(
                        out=mo[:, dt_, tt * 128:(tt + 1) * 128], in_=banks[bk][:, tt * 128:(tt + 1) * 128], func=AF.Copy,
                        scale=g2(dt_)), reads=[R_bank[bk], R_mod], writes=[r_mo[tt]])
            for tt in range(4):
                tcol = (c0 + tt * 128) // 128
                b = tt % 2
                P.dma("sp", xrow[b], xm_s[tcol * 128:(tcol + 1) * 128, :], r_xrow[b], reads=[R_xm], writes=[r_xrow[b]])
                for g in range(4):
                    bk = 6 + g % 2
                    for q in range(4):
                        dt_ = g * 4 + q
                        P.op("pe", lambda e, bk=bk, q=q, dt_=dt_, tt=tt: e.transpose(
                            out=banks[bk][:, q * 128:(q + 1) * 128], in_=mo[:, dt_, tt * 128:(tt + 1) * 128], identity=ident_f[:]),
                            reads=[r_mo[tt], R_const], writes=[R_bank[bk]])
                    P.op("dve", lambda e, b=b, bk=bk, g=g: e.tensor_tensor(out=xo[b][:, g * 512:(g + 1) * 512], in0=banks[bk][:, :],
                                                                          in1=xrow[b][:, g * 512:(g + 1) * 512], op=ALU.add),
                         reads=[R_bank[bk], r_xrow[b]], writes=[r_xo[b]])
                r0 = (tcol - 1) * 128
                P.dma("sp", out[r0:r0 + 128, :], xo[b], r_xo[b], reads=[r_xo[b]], writes=[R_out])

    def dbg_dump():
        dm = nc.dram_tensor("dbg_mod", [128, 96], F32, kind="ExternalOutput").ap()
        r = P.res("dbgm")
        P.dma("sp", dm[:, :], modv[:], r, reads=[R_mod], writes=[R_out])
        dk = nc.dram_tensor("dbg_kmean", [128, H * 16], F32, kind="ExternalOutput").ap()
        r2 = P.res("dbgk")
        P.dma("sp", dk[:, :], kmean[:], r2, reads=[R_kmean], writes=[R_out])

    phases = [phase0, phase1, phase2, phase3, phase4, phase5]
    for i, ph in enumerate(phases):
        if i > STOP:
            break
        ph()
        P.barrier()
    if DEBUG:
        dbg_dump()
    P.barrier()
    P.emit()
    return nc, P


def _host_consts(half):
    ident = np.eye(128, dtype=np.float32)
    kk = np.arange(128)[:, None]
    qq = np.arange(128)[None, :]
    caus = np.where(kk <= qq, 0.0, -BIG).astype(np.float32)
    eoh = np.zeros((16, 16 * 128), np.float32)
    for n in range(16):
        eoh[n, n * 128:(n + 1) * 128] = 1.0
    pmask = np.zeros((128, NW, 16), np.float32)
    pbias = np.zeros((128, NW, 16), np.float32)
    for i in range(NW):
        cur = (W0 + i) // 2
        for n in range(16):
            valid = n < cur and (half == 1 or n >= 8)
            if not valid:
                pmask[:, i, n] = -1e30
                pbias[:, i, n] = -BIG
    invf = (np.float32(10000.0) ** (-np.arange(64, dtype=np.float32) / np.float32(64))).astype(np.float32)
    cst = np.zeros((128, 4), np.float32)
    cst[:, 0] = float(half)
    cst[:, 1] = np.concatenate([invf, invf])
    cst[:, 2] = np.concatenate([np.ones(64, np.float32), -np.ones(64, np.float32)])
    return dict(ident=ident, caus=caus, eoh=eoh, pmask=pmask.reshape(128, -1), pbias=pbias.reshape(128, -1), cst=cst)


def _fm(v, n):
    return np.ascontiguousarray(np.asarray(v, np.float32).reshape(n, 128).T)


def make_in_maps(inp):
    x = np.asarray(inp["x"], np.float32)
    c = np.asarray(inp["c"], np.float32)
    pos = np.asarray(inp["positions"], np.int32)
    L = 0
    shared = {}
    shared["w_mod"] = np.ascontiguousarray(inp["w_mod"][L], dtype=np.float32)
    shared["bmodT"] = _fm(inp["b_mod"][L], 96)
    shared["n1gT"] = _fm(inp["norm1_g"][L], 16)
    shared["n2gT"] = _fm(inp["norm2_g"][L], 16)
    shared["w_in"] = np.ascontiguousarray(inp["w_in"][L], dtype=np.float32)
    shared["qkg"] = np.ascontiguousarray(np.stack([inp["q_norm_g"][L], inp["k_norm_g"][L]], axis=1).astype(np.float32))
    lre = np.asarray(inp["ssm_lambda_re"][L], np.float32)
    lim = np.asarray(inp["ssm_lambda_im"][L], np.float32)
    ldt = np.asarray(inp["ssm_log_dt"][L], np.float32)
    ldt_gp = np.repeat(ldt[:, None], 64, axis=1)
    shared["lre_b"] = np.ascontiguousarray(np.broadcast_to(lre.reshape(1, 4096), (128, 4096)))
    shared["lim_b"] = np.ascontiguousarray(np.broadcast_to(lim.reshape(1, 4096), (128, 4096)))
    shared["ldt_b"] = np.ascontiguousarray(np.broadcast_to(ldt_gp.reshape(1, 4096), (128, 4096)))

    def pl(a):
        return np.ascontiguousarray(a.reshape(32, 2, 64).transpose(1, 2, 0).reshape(128, 32))

    shared["lre_p"] = pl(lre)
    shared["lim_p"] = pl(lim)
    shared["ldt_p"] = pl(ldt_gp)
    bre = np.asarray(inp["ssm_b_re"][L], np.float32)
    bim = np.asarray(inp["ssm_b_im"][L], np.float32)

    def bl(bm):
        o = np.zeros((128, 32, 2, 64), np.float32)
        for pr in range(32):
            for g2 in range(2):
                g = 2 * pr + g2
                g8 = g % 8
                o[g8 * 16:(g8 + 1) * 16, pr, g2, :] = bm[g].T
        return np.ascontiguousarray(o.reshape(128, 4096))

    shared["bre_l"] = bl(bre)
    shared["bim_l"] = bl(bim)
    cre = np.asarray(inp["ssm_c_re"][L], np.float32)
    cim = np.asarray(inp["ssm_c_im"][L], np.float32)

    def cbd(cm):
        o = np.zeros((2, 64, 32, 2, 16), np.float32)
        for pr in range(32):
            for g2 in range(2):
                o[g2, :, pr, g2, :] = cm[2 * pr + g2].T
        return np.ascontiguousarray(o.reshape(128, 32 * 32))

    shared["cre_bd"] = cbd(cre)
    shared["cim_bd"] = cbd(cim)
    shared["dT"] = _fm(inp["ssm_d"][L], 8)
    shared["w_glu"] = np.ascontiguousarray(inp["w_glu"][L], dtype=np.float32)
    shared["bgluT"] = _fm(inp["b_glu"][L], 8)
    shared["w_abr"] = np.ascontiguousarray(inp["w_attn_br"][L], dtype=np.float32)
    shared["w_sbr"] = np.ascontiguousarray(inp["w_ssm_br"][L], dtype=np.float32)
    shared["w_out"] = np.ascontiguousarray(inp["w_out"][L], dtype=np.float32)
    shared["w_up"] = np.ascontiguousarray(inp["w_up"][L], dtype=np.float32)
    cw = np.asarray(inp["conv_w"][L], np.float32)
    shared["cwT"] = np.ascontiguousarray(np.stack([_fm(cw[k], 88) for k in range(3)], axis=1).reshape(128, 3 * 88))
    shared["cbT"] = _fm(inp["conv_b"][L], 88)
    shared["w_down"] = np.ascontiguousarray(inp["w_down"][L], dtype=np.float32)
    in_maps = []
    for core in range(8):
        b, half = core // 2, core % 2
        m = dict(shared)
        if half == 1:
            xs, ps_ = x[b], pos[b]
        else:
            xs = np.concatenate([x[b, 2048:], x[b, :2048]], axis=0)
            ps_ = np.concatenate([pos[b, 2048:], pos[b, :2048]], axis=0)
        m["xin"] = np.ascontiguousarray(xs)
        m["posb"] = np.ascontiguousarray(np.broadcast_to(ps_.reshape(1, S), (128, S)).astype(np.int32))
        m["cT"] = _fm(c[b], 16)
        m.update(_host_consts(half))
        in_maps.append(m)
    return in_maps


_CACHE = {}


def kernel(**inputs):
    in_maps = make_in_maps(inputs)
    if "nc" not in _CACHE:
        _CACHE["nc"] = build_program()[0]
    nc = _CACHE["nc"]
    res = run_bass_kernel_spmd(nc, in_maps, core_ids=list(range(8)))
    _CACHE["last"] = res
    outp = np.zeros((4, S, D), np.float32)
    for core in range(8):
        b, half = core // 2, core % 2
        outp[b, half * 2048:(half + 1) * 2048] = res.results[core]["out"]
    return outp
```
